# Optimizing a Trainium2 kernel written in Bass

```python
import jax, jax.numpy as jnp
from jax import lax
import numpy as np

D_MODEL = 1024
BATCH = 8
SEQ = 2048
DEPTH = 1
DEC_BATCH = 128
DEC_SEQ = 8
PAST_LEN = 16384
PAGE_SIZE = 128

POOL_WINDOWS = (2, 4, 8, 16)
N_POOL_GROUPS = len(POOL_WINDOWS)
POOL_GROUP = D_MODEL // 8
D_POOL = N_POOL_GROUPS * POOL_GROUP
POOL_BUF = max(POOL_WINDOWS) - 1
HEAD_DIM = 128
D_HGRN = D_MODEL
N_HEADS = D_HGRN // HEAD_DIM
CHUNK = 64
D_FF = ((8 * D_MODEL // 3 + 255) // 256) * 256
D_IN = D_POOL + 4 * D_HGRN + 2 * D_MODEL
ALPHA = (2.0 * DEPTH) ** 0.25
BETA = (8.0 * DEPTH) ** -0.25
LN_EPS = 1e-5
RMS_EPS = 1e-6

kernel_name = "pool_hgrn2_gated_hybrid_step"


def layer_norm(x, g, b):
    xf = x.astype(jnp.float32)
    mu = jnp.mean(xf, axis=-1, keepdims=True)
    var = jnp.mean(jnp.square(xf - mu), axis=-1, keepdims=True)
    return ((xf - mu) * lax.rsqrt(var + LN_EPS) * g + b).astype(x.dtype)


def pool_mixer(u, buf, start, w_pool_grp, pool_scale):
    L = u.shape[1]
    ext = jnp.concatenate([buf.astype(u.dtype), u], axis=1).astype(jnp.float32)
    cs = jnp.concatenate([jnp.zeros_like(ext[:, :1]), jnp.cumsum(ext, axis=1)], axis=1)
    end = cs[:, POOL_BUF + 1:]
    pos = start + jnp.arange(L)
    uf = u.astype(jnp.float32)
    outs = []
    for gi, w in enumerate(POOL_WINDOWS):
        sl = slice(gi * POOL_GROUP, (gi + 1) * POOL_GROUP)
        begin = cs[:, POOL_BUF + 1 - w: POOL_BUF + 1 - w + L, sl]
        cnt = jnp.minimum(pos + 1, w).astype(jnp.float32)[None, :, None]
        mean = (end[..., sl] - begin) / cnt
        outs.append(jnp.einsum('blc,cd->bld', mean - uf[..., sl], w_pool_grp[gi]))
    y = jnp.concatenate(outs, axis=-1) * pool_scale
    return y.astype(u.dtype), ext[:, -POOL_BUF:].astype(u.dtype)


def chunked_gated_recurrence(q, k, v, logf, s0):
    B, L, H, K = q.shape
    V = v.shape[-1]
    C = CHUNK if L % CHUNK == 0 else L
    N = L // C
    rs = lambda t: t.reshape(B, N, C, H, t.shape[-1])
    q, k, v, logf = rs(q), rs(k), rs(v), rs(logf)
    b = jnp.cumsum(logf, axis=2)
    qd = q * jnp.exp(b)
    kd = k * jnp.exp(-b)
    kt = k * jnp.exp(b[:, :, -1:] - b)
    causal = jnp.tril(jnp.ones((C, C), dtype=bool))
    att = jnp.where(causal, jnp.einsum('bnthk,bnshk->bnhts', qd, kd), 0.0)
    o_intra = jnp.einsum('bnhts,bnshv->bnthv', att, v)
    chunk_decay = jnp.moveaxis(jnp.exp(b[:, :, -1]), 1, 0)
    chunk_kv = jnp.moveaxis(jnp.einsum('bnshk,bnshv->bnhkv', kt, v), 1, 0)

    def step(s, inp):
        dec, kv = inp
        return dec[..., None] * s + kv, s

    s_final, s_starts = lax.scan(step, s0, (chunk_decay, chunk_kv))
    o_inter = jnp.einsum('bnthk,nbhkv->bnthv', qd, s_starts)
    return (o_intra + o_inter).reshape(B, L, H, V), s_final


def hgrn2_mixer(z_q, z_f, z_i, z_og, s0, lb, hgrn_norm_w):
    B, L, _ = z_q.shape
    f32 = jnp.float32
    heads = lambda t: t.reshape(B, L, N_HEADS, HEAD_DIM)
    q = heads(jax.nn.silu(z_q.astype(f32)))
    f = lb + (1.0 - lb) * jax.nn.sigmoid(z_f.astype(f32))
    k = heads(1.0 - f)
    logf = heads(jnp.log(f))
    v = heads(z_i.astype(f32))
    o, s_new = chunked_gated_recurrence(q, k, v, logf, s0.astype(f32))
    o = o * lax.rsqrt(jnp.mean(jnp.square(o), axis=-1, keepdims=True) + RMS_EPS)
    o = o.reshape(B, L, D_HGRN) * hgrn_norm_w * jax.nn.silu(z_og.astype(f32))
    return o.astype(z_q.dtype), s_new.astype(s0.dtype)


def trunk_layer(x, c, pool_buf, hgrn_s, start, lb,
                w_ada, b_ada, w_in, w_pool_grp, pool_scale, hgrn_norm_w,
                w_a, w_b, w_out, ln1_g, ln1_b, w_gate, w_up, w_down, ln2_g, ln2_b):
    mod = jax.nn.silu(c) @ w_ada + b_ada
    sh1, sc1, g1, sh2, sc2, g2 = jnp.split(mod[:, None, :], 6, axis=-1)
    u = x * (1 + sc1) + sh1
    z = u @ w_in
    z_pool, z_q, z_f, z_i, z_og, z_ga, z_gb = jnp.split(
        z, [D_POOL, D_POOL + D_HGRN, D_POOL + 2 * D_HGRN, D_POOL + 3 * D_HGRN,
            D_POOL + 4 * D_HGRN, D_POOL + 4 * D_HGRN + D_MODEL], axis=-1)
    y_pool, new_buf = pool_mixer(z_pool, pool_buf, start, w_pool_grp, pool_scale)
    y_hgrn, new_s = hgrn2_mixer(z_q, z_f, z_i, z_og, hgrn_s, lb, hgrn_norm_w)
    merged = jax.nn.sigmoid(z_ga) * (y_pool @ w_a) + jax.nn.sigmoid(z_gb) * (y_hgrn @ w_b)
    x = layer_norm(ALPHA * x + g1 * (merged @ w_out), ln1_g, ln1_b)
    h = x * (1 + sc2) + sh2
    ffn = (jax.nn.silu(h @ w_gate) * (h @ w_up)) @ w_down
    x = layer_norm(ALPHA * x + g2 * ffn, ln2_g, ln2_b)
    return x, new_buf, new_s


def setup_inputs(seed: int = 0) -> dict:
    key = jax.random.key(seed)
    ks = jax.random.split(key, 24)
    nrm = lambda k, shape, s: jax.random.normal(k, shape, jnp.float32) * s
    Dl = DEPTH
    return {
        "x_prompt": nrm(ks[0], (BATCH, SEQ, D_MODEL), 1.0),
        "x_sample": nrm(ks[1], (DEC_BATCH, DEC_SEQ, D_MODEL), 1.0),
        "state_pool": nrm(ks[2], (Dl, DEC_BATCH, POOL_BUF, D_POOL), 1.0),
        "state_hgrn": nrm(ks[3], (Dl, DEC_BATCH, N_HEADS, HEAD_DIM, HEAD_DIM), 0.5),
        "c_prompt": nrm(ks[4], (BATCH, D_MODEL), 1.0),
        "c_sample": nrm(ks[5], (DEC_BATCH, D_MODEL), 1.0),
        "w_ada": nrm(ks[6], (Dl, D_MODEL, 6 * D_MODEL), 0.5 * D_MODEL ** -0.5),
        "b_ada": nrm(ks[7], (Dl, 6 * D_MODEL), 0.01),
        "w_in": nrm(ks[8], (Dl, D_MODEL, D_IN), D_MODEL ** -0.5),
        "w_pool_grp": nrm(ks[9], (Dl, N_POOL_GROUPS, POOL_GROUP, POOL_GROUP), POOL_GROUP ** -0.5),
        "pool_scale": 1.0 + nrm(ks[10], (Dl, D_POOL), 0.1),
        "lb_logits": nrm(ks[11], (Dl + 1, D_HGRN), 0.1),
        "hgrn_norm_w": 1.0 + nrm(ks[12], (Dl, D_HGRN), 0.1),
        "w_a": nrm(ks[13], (Dl, D_POOL, D_MODEL), BETA * D_POOL ** -0.5),
        "w_b": nrm(ks[14], (Dl, D_HGRN, D_MODEL), BETA * D_HGRN ** -0.5),
        "w_out": nrm(ks[15], (Dl, D_MODEL, D_MODEL), BETA * D_MODEL ** -0.5),
        "ln1_g": 1.0 + nrm(ks[16], (Dl, D_MODEL), 0.1),
        "ln1_b": nrm(ks[17], (Dl, D_MODEL), 0.01),
        "w_gate": nrm(ks[18], (Dl, D_MODEL, D_FF), D_MODEL ** -0.5),
        "w_up": nrm(ks[19], (Dl, D_MODEL, D_FF), D_MODEL ** -0.5),
        "w_down": nrm(ks[20], (Dl, D_FF, D_MODEL), BETA * D_FF ** -0.5),
        "ln2_g": 1.0 + nrm(ks[21], (Dl, D_MODEL), 0.1),
        "ln2_b": nrm(ks[22], (Dl, D_MODEL), 0.01),
    }


def reference(x_prompt, x_sample, state_pool, state_hgrn, c_prompt, c_sample,
              w_ada, b_ada, w_in, w_pool_grp, pool_scale, lb_logits, hgrn_norm_w,
              w_a, w_b, w_out, ln1_g, ln1_b, w_gate, w_up, w_down, ln2_g, ln2_b):
    lb_all = jnp.cumsum(jax.nn.softmax(lb_logits.astype(jnp.float32), axis=0), axis=0)
    xp, xs = x_prompt, x_sample
    pool_p, hgrn_p, pool_s, hgrn_s = [], [], [], []
    for l in range(DEPTH):
        params = (w_ada[l], b_ada[l], w_in[l], w_pool_grp[l], pool_scale[l], hgrn_norm_w[l],
                  w_a[l], w_b[l], w_out[l], ln1_g[l], ln1_b[l], w_gate[l], w_up[l], w_down[l],
                  ln2_g[l], ln2_b[l])
        buf0 = jnp.zeros((xp.shape[0], POOL_BUF, D_POOL), xp.dtype)
        s0 = jnp.zeros((xp.shape[0], N_HEADS, HEAD_DIM, HEAD_DIM), state_hgrn.dtype)
        xp, bp, sp = trunk_layer(xp, c_prompt, buf0, s0, 0, lb_all[l], *params)
        xs, bs, ss = trunk_layer(xs, c_sample, state_pool[l], state_hgrn[l], PAST_LEN, lb_all[l], *params)
        pool_p.append(bp)
        hgrn_p.append(sp)
        pool_s.append(bs)
        hgrn_s.append(ss)
    return (xp, xs, jnp.stack(pool_p), jnp.stack(hgrn_p), jnp.stack(pool_s), jnp.stack(hgrn_s))
```

```python
import numpy as np
from contextlib import ExitStack
import concourse.bass as bass
import concourse.mybir as mybir
from concourse.bass_utils import run_bass_kernel_spmd

F32 = mybir.dt.float32
BF16 = mybir.dt.bfloat16
AF = mybir.ActivationFunctionType
ALU = mybir.AluOpType

ALPHA = 2.0 ** 0.25
LN_EPS = 1e-5
RMS_EPS = 1e-6
NCORE = 8
TL = 768
NPASS = 3
PASSES = [(0, 768), (768, 768), (1536, 512)]
RSTD_MODE = "tokb"
ZIP = True
KVMODE = "alt"
USE_SCR = True
STAGGER = 32


class Tok:
    __slots__ = ("name", "lw", "rd", "dsem", "dcount")

    def __init__(self, name):
        self.name = name
        self.lw = None
        self.rd = {}
        self.dsem = None
        self.dcount = 0


class Op:
    __slots__ = ("eng", "fn", "deps", "inc", "val", "dtok", "dval")

    def __init__(self, eng, fn):
        self.eng = eng
        self.fn = fn
        self.deps = []
        self.inc = False
        self.val = 0
        self.dtok = None
        self.dval = 0


class Trk:
    ENGS = ("pe", "act", "dve", "pool", "sp")

    def __init__(self, nc):
        self.nc = nc
        self.ops = {e: [] for e in self.ENGS}
        self.dtoks = []

    def tok(self, name="t"):
        return Tok(name)

    def _collect(self, eng, r, w):
        deps = {}

        def add(ref, raw):
            if ref is None:
                return
            if ref[0] == "e":
                o2 = ref[1]
                if o2.eng == eng and eng == "pe":
                    return
                deps[id(o2)] = ref
            else:
                key = ("d", id(ref[1]))
                if key not in deps or deps[key][2] < ref[2]:
                    deps[key] = ref

        for t in r:
            add(t.lw, True)
        for t in w:
            add(t.lw, False)
            for x in t.rd.values():
                add(x, False)
        return list(deps.values())

    def op(self, eng, fn, r=(), w=()):
        o = Op(eng, fn)
        o.deps = self._collect(eng, r, w)
        for d in o.deps:
            if d[0] == "e":
                d[1].inc = True
        ref = ("e", o)
        for t in w:
            t.lw = ref
            t.rd = {}
        for t in r:
            t.rd[eng] = ref
        self.ops[eng].append(o)
        return o

    def dma(self, out, in_, tok, write, r=(), w=()):
        q = "sp"
        rr = list(r) + ([] if write else [tok])
        ww = list(w) + ([tok] if write else [])
        o = Op(q, lambda e: e.dma_start(out=out, in_=in_))
        o.deps = self._collect(q, rr, ww)
        for d in o.deps:
            if d[0] == "e":
                d[1].inc = True
        if tok.dsem is None:
            tok.dsem = "pending"
            self.dtoks.append(tok)
        tok.dcount += 16
        o.dtok = tok
        o.dval = tok.dcount
        ref = ("d", tok, tok.dcount)
        for t in ww:
            t.lw = ref
            t.rd = {}
        for t in rr:
            t.rd[("d", id(tok))] = ref
        self.ops[q].append(o)
        return o

    def emit(self, es, final_toks=()):
        nc = self.nc
        esem = {e: es.enter_context(nc.semaphore("s_" + e)) for e in self.ENGS}
        for i, t in enumerate(self.dtoks):
            t.dsem = es.enter_context(nc.semaphore("d%d" % i))
        for e in self.ENGS:
            c = 0
            for o in self.ops[e]:
                if o.inc:
                    c += 1
                    o.val = c
        fin = Op("sp", None)
        fin.deps = [("d", t, t.dcount) for t in final_toks]
        self.ops["sp"].append(fin)
        block = es.enter_context(nc.Block())
        ops = self.ops

        def run(e, name):
            seen = {}
            for o in ops[name]:
                for d in o.deps:
                    if d[0] == "e":
                        sem, val, key = esem[d[1].eng], d[1].val, d[1].eng
                    else:
                        sem, val, key = d[1].dsem, d[2], id(d[1])
                    if seen.get(key, 0) >= val:
                        continue
                    e.wait_ge(sem, val)
                    seen[key] = val
                if o.fn is None:
                    continue
                ins = o.fn(e)
                if o.dtok is not None:
                    ins.then_inc(o.dtok.dsem, 16)
                elif o.inc:
                    ins.then_inc(esem[name], 1)

        @block.tensor
        def _(e):
            run(e, "pe")

        @block.scalar
        def _(e):
            run(e, "act")

        @block.vector
        def _(e):
            run(e, "dve")

        @block.gpsimd
        def _(e):
            run(e, "pool")

        @block.sync
        def _(e):
            run(e, "sp")


def build_nc():
    nc = bass.Bass("TRN2", target_bir_lowering=False)
    din = lambda n, s: nc.dram_tensor(n, list(s), F32, kind="ExternalInput").ap()
    dout = lambda n, s: nc.dram_tensor(n, list(s), F32, kind="ExternalOutput").ap()
    d_xT = din("xT", [128, 8, 2176])
    d_cT = din("cT", [128, 8, 17])
    d_badaT = din("badaT", [128, 48])
    d_wada = din("wada", [24, 128, 2048])
    d_win = din("win", [26, 128, 2048])
    d_wpg = din("wpg", [128, 512])
    d_wa = din("wa", [4, 128, 1024])
    d_wb = din("wb", [4, 128, 2048])
    d_wo = din("wo", [4, 128, 2048])
    d_wg = din("wg", [11, 128, 2048])
    d_wu = din("wu", [11, 128, 2048])
    d_wd = din("wd", [16, 128, 1408])
    d_small = din("small", [128, 64])
    d_consts = din("consts", [128, 464])
    d_spT = din("spT", [128, 4, 16, 15])
    d_shg = din("shg", [16, 8, 128, 128])
    o_yT = dout("yT", [128, 8, 2176])
    o_zpp = dout("zpp", [128, 4, 15])
    o_zps = dout("zps", [128, 4, 16, 15])
    o_nhp = dout("nhp", [8, 128, 128])
    o_nhs = dout("nhs", [16, 8, 128, 128])

    NSCR = 76
    d_scr = nc.dram_tensor("wscr", [NSCR, 128, 2048], BF16, kind="Internal").ap()

    es = ExitStack()
    with es:
        K = Trk(nc)
        T = K.tok
        sb = lambda n, s, d=F32: es.enter_context(nc.sbuf_tensor(n, list(s), d))
        ARENA_F = 53200
        arena = sb("arena", [128, ARENA_F], F32)
        apos = [0]

        def carve_at(off, nel, dt):
            nb = nel * (2 if dt == BF16 else 4)
            assert off % 4 == 0
            a = arena[:, off // 4:(off + nb + 3) // 4]
            if dt == BF16:
                a = a.bitcast(BF16)[:, :nel]
            return a

        def carve(nel, dt):
            off = apos[0]
            nb = (nel * (2 if dt == BF16 else 4) + 63) // 64 * 64
            apos[0] += nb
            assert apos[0] <= ARENA_F * 4, "arena overflow"
            return carve_at(off, nel, dt)

        v3 = lambda a, n0: a.rearrange("p (a b) -> p a b", a=n0)
        UT = v3(carve(8 * TL, BF16), 8); tUT = T("UT")
        HT = UT
        tHTg = {c_: T("HT%d" % c_) for c_ in (0, 512)}
        tHT_all = list(tHTg.values())
        X1 = v3(carve(8 * TL, F32), 8); tX1 = T("X1")
        XS = [carve(TL, F32) for _ in range(3)]; tXS = [T("XS%d" % i) for i in range(3)]
        at_off = apos[0]
        AT = v3(carve(22 * TL, BF16), 22); tAT = T("AT")
        apos[0] = at_off
        YHG = v3(carve(8 * TL, BF16), 8); tYHG = T("YHG")
        MRG = v3(carve(8 * TL, BF16), 8); tMRG = T("MRG")
        YPOOL = v3(carve(4 * TL, BF16), 4); tYPOOL = T("YPOOL")
        apos[0] = max(apos[0], at_off + 22 * TL * 2)
        rc_off = apos[0]
        class HSet:
            pass

        def mk_set(wd, nt, nm):
            S_ = HSet()
            S_.Fa = carve(wd, F32); S_.tFa = T(nm + "Fa")
            S_.Fk = carve(wd, F32); S_.tFk = T(nm + "Fk")
            S_.P = carve(wd, F32); S_.tP = T(nm + "P")
            S_.rP = carve(wd, F32); S_.trP = T(nm + "rP")
            S_.Qs = carve(wd, F32); S_.tQs = T(nm + "Qs")
            S_.KD = carve(wd, BF16); S_.tKD = T(nm + "KD")
            S_.KT = carve(wd, BF16); S_.tKT = T(nm + "KT")
            S_.QD = carve(wd, BF16); S_.tQD = T(nm + "QD")
            S_.VT = v3(carve(wd, BF16), nt); S_.tVT = T(nm + "VT")
            S_.SG = carve(wd, F32); S_.tSG = T(nm + "SG")
            S_.AM = v3(carve(wd, BF16), nt); S_.tAM = T(nm + "AM")
            S_.KTT = v3(carve(wd, BF16), nt); S_.tKTT = T(nm + "KTT")
            S_.O32 = S_.rP; S_.tO32 = S_.trP
            S_.OSQ = S_.KT; S_.tOSQ = S_.tKT
            S_.BM = v3(S_.Qs, nt); S_.tBM = S_.tQs
            S_.tBM2 = T(nm + "BM2")
            S_.RSs = carve(16, F32); S_.tRSs = T(nm + "RSs")
            S_.D1 = carve(wd, F32); S_.tD1 = T(nm + "D1")
            S_.toks = [S_.tFa, S_.tFk, S_.tP, S_.trP, S_.tQs, S_.tKD, S_.tKT, S_.tQD, S_.tVT, S_.tSG, S_.tAM, S_.tKTT, S_.tRSs, S_.tD1]
            return S_

        HS = [mk_set(512, 4, "A"), mk_set(512, 4, "B")]
        SS = mk_set(128, 1, "S")
        for i_, S_ in enumerate(HS):
            S_.SBFC = v3(carve(9 * 128, BF16), 9); S_.tSBFC = T("SBFC%d" % i_)
            S_.kvb = (4 + 2 * i_, 5 + 2 * i_)
            S_.sidx = i_
            S_.toks.append(S_.tSBFC)
        S0 = v3(carve(16 * 128, F32), 16); tS0 = T("S0")
        S0B = v3(carve(16 * 128, BF16), 16); tS0B = T("S0B")
        VEXP = v3(carve(16 * 128, BF16), 16); tVEXP = T("VEXP")
        hg_toks = HS[0].toks + HS[1].toks + SS.toks + [tS0, tS0B, tVEXP]
        rc_end = apos[0]
        apos[0] = rc_off
        ZP = carve(528, F32); tZP = T("ZP")
        PA = carve(528, F32); tPA = T("PA")
        PB = carve(528, F32); tPB = T("PB")
        ZPS = v3(carve(16 * 24, F32), 16); tZPS = T("ZPS")
        PAS = v3(carve(16 * 24, F32), 16); tPAS = T("PAS")
        PBS = v3(carve(16 * 24, F32), 16); tPBS = T("PBS")
        MT = carve(TL, BF16); tMT = T("MT")
        T16 = carve(16, F32); tT16 = T("T16")
        pl_toks = [tZP, tPA, tPB, tZPS, tPAS, tPBS, tMT, tT16]
        rc_end = max(rc_end, apos[0])
        apos[0] = rc_off
        RB = v3(carve(8 * 512, BF16), 8); tRB = T("RB")
        SQ = v3(carve(8 * 512, BF16), 8); tSQ = T("SQ")
        ST = [carve(512, F32) for _ in range(4)]; tST = [T("ST%d" % i) for i in range(4)]
        TA = [carve(512, F32) for _ in range(4)]; tTA = [T("TA%d" % i) for i in range(4)]
        TM = [carve(512, F32) for _ in range(2)]; tTM = [T("TM%d" % i) for i in range(2)]
        tRBk = [T("RB%d" % i) for i in range(8)]; tSQk = [T("SQ%d" % i) for i in range(8)]
        ln_toks = tRBk + tSQk + tST + tTA + tTM
        rc_end = max(rc_end, apos[0])
        apos[0] = rc_end
        STG = [carve(2048, F32) for _ in range(2)]; tSTG = [T("STG%d" % i) for i in range(2)]
        WBF = [carve(2048, BF16) for _ in range(6)]; tWBF = [T("WBF%d" % i) for i in range(6)]
        CONS = carve(464, F32); tCONS = T("CONS")
        SMALL = carve(64, F32); tSMALL = T("SMALL")
        IDB = carve(128, BF16); ONESB = carve(128, BF16); tCB = T("CB")
        SEQMB = carve(16, BF16)
        NHALF = carve(8, F32); ZERO = carve(64, F32); DUMMY = carve(8, F32); tDUM = T("DUM")
        ONES32 = carve(128, F32)
        CTF = v3(carve(8 * 17, F32), 8); tCTF = T("CTF")
        SCT = v3(carve(8 * 17, BF16), 8); tSCT = T("SCT")
        BADAT = carve(48, F32); tBADAT = T("BADAT")
        MODT = v3(carve(48 * 17, F32), 48); tMODT = T("MODT")
        CO = carve(40, F32); tCO = T("CO")
        WPGS = carve(512, F32); tWPGS = T("WPGS")
        WPG = v3(carve(512, BF16), 4); tWPG = T("WPG")
        HIST = v3(carve(4 * 16, F32), 4); tHIST = T("HIST")
        S32 = v3(carve(8 * 128, F32), 8); tS32 = T("S32")
        S32b = v3(carve(8 * 128, F32), 8)
        S32x = [S32, S32b]
        tS32h = [[T("S32_%d_%d" % (b_, h_)) for h_ in range(8)] for b_ in range(2)]
        print("arena bytes used per partition:", apos[0])

        MASKP = CONS[:, 128:256]
        MASKS = CONS[:, 256:384]
        INVC = v3(CONS[:, 400:464], 4)
        PSC = SMALL[:, 0:4]
        LG1, LB1, LG2, LB2 = SMALL[:, 28:36], SMALL[:, 36:44], SMALL[:, 44:52], SMALL[:, 52:60]
        C0, C1, NC1, NW2 = CO[:, 0:8], CO[:, 8:16], CO[:, 16:24], CO[:, 24:32]

        PS = [es.enter_context(nc.psum_tensor("PS%d" % i, [128, 512], F32)) for i in range(8)]
        tPS = [T("PS%d" % i) for i in range(8)]
        prot = [0]

        pnb = [4]

        def pbank():
            i = prot[0] % pnb[0]
            prot[0] += 1
            return PS[i], tPS[i]

        def fence(toks):
            K.op("dve", lambda e: e.memset(DUMMY[:, 0:1], 0.0), (), list(toks) + [tDUM])

        def TS(out, in0, s1, s2, op0, op1=None, r=(), w=(), eng="dve"):
            if op1 is None:
                K.op(eng, lambda e: e.tensor_scalar(out=out, in0=in0, scalar1=s1, scalar2=None, op0=op0), r, w)
            else:
                K.op(eng, lambda e: e.tensor_scalar(out=out, in0=in0, scalar1=s1, scalar2=s2, op0=op0, op1=op1), r, w)

        def TT(out, in0, in1, op, r=(), w=(), eng="dve"):
            K.op(eng, lambda e: e.tensor_tensor(out=out, in0=in0, in1=in1, op=op), r, w)

        def STT(out, in0, sc, in1, op0, op1, r=(), w=()):
            K.op("dve", lambda e: e.scalar_tensor_tensor(out=out, in0=in0, scalar=sc, in1=in1, op0=op0, op1=op1), r, w)

        def ACT(out, in_, func, r=(), w=(), scale=1.0, bias=None):
            if bias is None:
                K.op("act", lambda e: e.activation(out=out, in_=in_, func=func, scale=scale), r, w)
            else:
                K.op("act", lambda e: e.activation(out=out, in_=in_, func=func, scale=scale, bias=bias), r, w)

        def MM(out, lhsT, rhs, start, stop, r=(), w=()):
            K.op("pe", lambda e: e.matmul(out, lhsT=lhsT, rhs=rhs, start=start, stop=stop), r, w)

        def TRP(out, in_, r=(), w=()):
            K.op("pe", lambda e: e.transpose(out=out, in_=in_, identity=IDB), r, w)

        def SCAN(out, d0, d1, r=(), w=()):
            K.op("dve", lambda e: e.tensor_tensor_scan(out=out, data0=d0, data1=d1, initial=1.0, op0=ALU.mult, op1=ALU.add), r, w)

        def CPY(out, in_, r=(), w=(), eng="dve"):
            K.op(eng, lambda e: e.tensor_copy(out=out, in_=in_), r, w)

        def MSET(out, val, w=(), eng="dve"):
            K.op(eng, lambda e: e.memset(out, val), (), w)

        I32 = mybir.dt.int32

        def RSQRT(y, x, t, hx, ty, tx, tt, thx, iters=3):
            TS(y.bitcast(I32), x.bitcast(I32), 1, None, ALU.arith_shift_right, r=[tx], w=[ty])
            TS(y.bitcast(I32), y.bitcast(I32), -1, 0x5f3759df, ALU.mult, ALU.add, r=[ty], w=[ty])
            TS(hx, x, -0.5, None, ALU.mult, r=[tx], w=[thx])
            for _ in range(iters):
                TT(t, y, y, ALU.mult, r=[ty], w=[tt])
                TT(t, t, hx, ALU.mult, r=[tt, thx], w=[tt])
                STT(y, t, 1.5, y, ALU.add, ALU.mult, r=[tt, ty], w=[ty])

        def RCP(out, in_, r=(), w=()):
            K.op("dve", lambda e: e.reciprocal(out=out, in_=in_), r, w)

        wq = []
        wloaded = []
        wctr = [0]

        sctr = [0]
        scr_tok = {}
        spill_q = []

        def flush_spill():
            _, sid_, nel_, b_, bt_ = spill_q.pop(0)
            K.dma(d_scr[sid_][:, :nel_], b_[:, :nel_], bt_, False, w=[scr_tok[sid_]])

        def _load(ap, nel, sid=None, from_scr=False):
            i = wctr[0]
            wctr[0] += 1
            b, bt = WBF[i % 6], tWBF[i % 6]
            if from_scr:
                while spill_q:
                    flush_spill()
                K.dma(b[:, :nel], d_scr[sid][:, :nel], bt, True, r=[scr_tok[sid]])
                return b, bt
            j = sctr[0]
            sctr[0] += 1
            s_, st = STG[j % 2], tSTG[j % 2]
            K.dma(s_[:, :nel], ap, st, True)
            ACT(b[:, :nel], s_[:, :nel], AF.Copy, r=[st], w=[bt])
            if sid is not None:
                scr_tok[sid] = T("scr%d" % sid)
                spill_q.append((i, sid, nel, b, bt))
            while spill_q and i - spill_q[0][0] >= 2:
                flush_spill()
            return b, bt

        def sched(lst):
            wq.extend(lst)

        def next_unit(lookahead=2):
            while len(wloaded) < 1 + lookahead and wq:
                wloaded.append(_load(*wq.pop(0)))
            return wloaded.pop(0)

        def proj(wv, wt, kc, c0, actv, at, col0, n):
            pb, pt = pbank()
            for k in range(kc):
                MM(pb[:, :n], wv[:, k, c0:c0 + 128], actv[:, k, col0:col0 + n], k == 0, k == kc - 1, r=[wt, at], w=[pt])
            return pb, pt

        K.dma(CONS, d_consts, tCONS, True)
        K.dma(SMALL, d_small, tSMALL, True)
        K.dma(CTF, d_cT, tCTF, True)
        K.dma(BADAT, d_badaT, tBADAT, True)
        K.dma(WPGS, d_wpg, tWPGS, True)
        CPY(IDB, CONS[:, 0:128], r=[tCONS], w=[tCB])
        MSET(ONESB, 1.0, w=[tCB])
        MSET(ONES32, 1.0, w=[tCB])
        CPY(SEQMB, CONS[:, 384:400], r=[tCONS], w=[tCB])
        MSET(NHALF, -0.5, w=[tCB])
        MSET(ZERO, 0.0, w=[tCB])
        MSET(S32, 0.0, w=tS32h[0])
        MSET(HIST, 0.0, w=[tHIST])
        ACT(WPG, v3(WPGS, 4), AF.Copy, r=[tWPGS], w=[tWPG])
        ACT(SCT, CTF, AF.Silu, r=[tCTF], w=[tSCT])
        TMP8 = CO[:, 32:40]
        TT(TMP8, SMALL[:, 4:12], SMALL[:, 12:20], ALU.subtract, r=[tSMALL], w=[tCO])
        ACT(TMP8, TMP8, AF.Tanh, r=[tCO], w=[tCO], scale=0.5)
        TS(C0, TMP8, 0.25, 0.75, ALU.mult, ALU.add, r=[tCO], w=[tCO])
        TS(C1, TMP8, -0.25, 0.25, ALU.mult, ALU.add, r=[tCO], w=[tCO])
        TS(NC1, TMP8, 0.25, -0.25, ALU.mult, ALU.add, r=[tCO], w=[tCO])
        TS(NW2, SMALL[:, 20:28], float(np.sqrt(128.0)), None, ALU.mult, r=[tSMALL], w=[tCO])

        def mod_units(u0, u1):
            for u in range(u0, u1):
                wb_, wt_ = next_unit()
                wv = v3(wb_, 8)
                for jj in range(2):
                    pk = 2 * u + jj
                    pb, pt = proj(wv, wt_, 8, jj * 128, SCT, tSCT, 0, 17)
                    TS(MODT[:, pk, :], pb[:, 0:17], BADAT[:, pk:pk + 1], None, ALU.add, r=[pt, tBADAT], w=[tMODT])

        sched([(d_wada[u], 2048, None, False) for u in range(8)])
        mod_units(0, 8)
        TS(MODT[:, 8:16, :], MODT[:, 8:16, :], 1.0, None, ALU.add, r=[tMODT], w=[tMODT])
        EPS2 = LN_EPS / (ALPHA * ALPHA)

        def mcol(pk):
            return MODT[:, pk, 0:1]

        def mrow(pk):
            return MODT[:, pk, 1:17].unsqueeze(2).to_broadcast([128, 16, 8])

        s3 = lambda a: a.rearrange("p (a b) -> p a b", a=16)

        def layer_norm_inplace(c0, n):
            xs = X1[:, :, c0:c0 + n]
            for k in range(8):
                ACT(RB[:, k, :n], X1[:, k, c0:c0 + n], AF.Copy, r=[tX1], w=[tRBk[k]])
                TT(SQ[:, k, :n], X1[:, k, c0:c0 + n], X1[:, k, c0:c0 + n], ALU.mult, r=[tX1], w=[tSQk[k]])
                MM(PS[6][:, :n], ONESB, RB[:, k, :n], k == 0, k == 7, r=[tCB, tRBk[k]], w=[tPS[6]])
                MM(PS[7][:, :n], ONESB, SQ[:, k, :n], k == 0, k == 7, r=[tCB, tSQk[k]], w=[tPS[7]])
            mean, msq, var, rstd = [a[:, :n] for a in ST]
            TS(mean, PS[6][:, :n], 1.0 / 1024, None, ALU.mult, r=[tPS[6]], w=[tST[0]])
            TT(msq, mean, mean, ALU.mult, r=[tST[0]], w=[tST[1]])
            STT(var, PS[7][:, :n], 1.0 / 1024, msq, ALU.mult, ALU.subtract, r=[tPS[7], tST[1]], w=[tST[2]])
            TS(var, var, EPS2, None, ALU.add, r=[tST[2]], w=[tST[2]])
            RSQRT(rstd, var, msq, TM[1][:, :n], tST[3], tST[2], tST[1], tTM[1])
            TT(xs, xs, mean.unsqueeze(1).to_broadcast([128, 8, n]), ALU.subtract, r=[tX1, tST[0]], w=[tX1])
            TT(xs, xs, rstd.unsqueeze(1).to_broadcast([128, 8, n]), ALU.mult, r=[tX1, tST[3]], w=[tX1])

        mctr = [0]
        pending_out = []

        xctr = [0]

        def load_x_chunk(k, q):
            i = xctr[0] % 3
            xctr[0] += 1
            b0, pl = PASSES[q]
            K.dma(XS[i][:, 0:pl], d_xT[:, k, b0:b0 + pl], tXS[i], True)
            if q == NPASS - 1:
                K.dma(XS[i][:, pl:pl + 128], d_xT[:, k, 2048:2176], tXS[i], True)
            return XS[i], tXS[i]

        def phase1(q):
            fence([tUT] + tHT_all)
            for k in range(8):
                xs_k, txs = load_x_chunk(k, q)
                pl = PASSES[q][1]
                TS(UT[:, k, 0:pl], xs_k[:, 0:pl], mcol(8 + k), mcol(k), ALU.mult, ALU.add, r=[txs, tMODT], w=[tUT])
                if q == NPASS - 1:
                    TT(s3(TM[0][:, 0:128]), s3(xs_k[:, pl:pl + 128]), mrow(8 + k), ALU.mult, r=[txs, tMODT], w=[tTM[0]])
                    TT(s3(UT[:, k, pl:pl + 128]), s3(TM[0][:, 0:128]), mrow(k), ALU.add, r=[tTM[0], tMODT], w=[tUT])

        def sched_pass(q_):
            seq = []
            for h in range(8):
                seq += [(d_win[2 + 2 * h], 2048), (d_win[3 + 2 * h], 2048)]
            ada_at = len(seq)
            seq += [(d_win[0], 2048), (d_win[1], 2048)]
            for u in range(4):
                seq += [(d_wa[u], 1024), (d_win[18 + u], 2048), (d_wb[u], 2048), (d_win[22 + u], 2048)]
            seq += [(d_wo[u], 2048) for u in range(4)]
            for u in range(11):
                seq += [(d_wg[u], 2048), (d_wu[u], 2048)]
            seq += [(d_wd[u], 1408) for u in range(16)]
            assert len(seq) == NSCR
            seq = [(ap, nel, sid, USE_SCR and q_ > 0) for sid, (ap, nel) in enumerate(seq)]
            if not USE_SCR:
                seq = [(ap, nel, None, False) for (ap, nel, _, _) in seq]
            if q_ == 0:
                seq = seq[:ada_at] + [(d_wada[u], 2048, None, False) for u in range(8, 24)] + seq[ada_at:]
            sched(seq)

        for ps_ in range(NPASS):
            last = ps_ == NPASS - 1
            base, plen = PASSES[ps_]
            ptgs = [(c_, min(512, plen - c_), False) for c_ in range(0, plen, 512)]
            tgs = ptgs + ([(plen, 128, True)] if last else [])
            sched_pass(ps_)

            if ps_ == 0:
                phase1(0)
            fence(ln_toks + hg_toks + [tAT, tYHG, tMRG, tYPOOL])
            pnb[0] = 4

            for S_ in HS + [SS]:
                MSET(S_.D1, 0.0, w=[S_.tD1], eng="pool")
            IDN32 = CONS[:, 0:128]
            units = {}

            def head_stream(h, B, c0, n, is_s):
                qf, tqf, vo, tvo = units[h]
                ntile = n // 128
                L, nch = (8, 16) if is_s else (64, n // 64)
                pf, tpf = proj(qf, tqf, 8, 128, UT, tUT, c0, n)
                ACT(B.Fa[:, :n], pf[:, :n], AF.Tanh, r=[tpf], w=[B.tFa], scale=0.5)
                cs_ = lambda a_: a_.rearrange("p (a b) -> p a b", a=nch)[:, :, 0:1]
                D1b, tD1b = (SS.D1, SS.tD1) if is_s else (B.D1, B.tD1)
                ACT(B.rP[:, :n], B.Fa[:, :n], AF.Identity, r=[B.tFa, tCO], w=[B.trP], scale=C1[:, h:h + 1], bias=C0[:, h:h + 1])
                ACT(cs_(D1b[:, :n]), cs_(B.rP[:, :n]), AF.Identity, r=[B.trP], w=[tD1b])
                ACT(cs_(B.rP[:, :n]), cs_(B.rP[:, :n]), AF.Identity, r=[B.trP], w=[B.trP], scale=0.0)
                yield
                SCAN(B.P[:, :n], B.rP[:, :n], D1b[:, :n], r=[B.trP, tD1b], w=[B.tP])
                pq, tpq = proj(qf, tqf, 8, 0, UT, tUT, c0, n)
                ACT(B.Qs[:, :n], pq[:, :n], AF.Silu, r=[tpq], w=[B.tQs]); yield
                ACT(B.Fk[:, :n], B.Fa[:, :n], AF.Identity, r=[B.tFa, tCO], w=[B.tFk], scale=NC1[:, h:h + 1], bias=C1[:, h:h + 1])
                pog, tpog = proj(vo, tvo, 8, 128, UT, tUT, c0, n)
                ACT(B.SG[:, :n], pog[:, :n], AF.Silu, r=[tpog], w=[B.tSG]); yield
                pv, tpv = pbank()
                for ti in range(ntile):
                    for k in range(8):
                        MM(pv[:, ti * 128:(ti + 1) * 128], UT[:, k, c0 + ti * 128:c0 + (ti + 1) * 128], vo[:, k, 0:128], k == 0, k == 7, r=[tUT, tvo], w=[tpv])
                ACT(B.VT[:, :ntile, :], v3(pv[:, :n], ntile), AF.Copy, r=[tpv], w=[B.tVT])
                yield
                RCP(B.rP[:, :n], B.P[:, :n], r=[B.tP], w=[B.trP]); yield
                TT(B.KD[:, :n], B.Fk[:, :n], B.rP[:, :n], ALU.mult, r=[B.tFk, B.trP], w=[B.tKD])
                c3 = lambda a_: a_.rearrange("p (a b) -> p a b", a=nch)
                TT(c3(B.KT[:, :n]), c3(B.KD[:, :n]), c3(B.P[:, :n])[:, :, L - 1:L].to_broadcast([128, nch, L]), ALU.mult, r=[B.tKD, B.tP], w=[B.tKT], eng="pool")
                TT(B.QD[:, :n], B.Qs[:, :n], B.P[:, :n], ALU.mult, r=[B.tQs, B.tP], w=[B.tQD])
                yield
                pat, tpat = pbank() if KVMODE == "batch" else (PS[4], tPS[4])
                for ti in range(ntile):
                    MM(pat[:, ti * 128:(ti + 1) * 128], B.KD[:, ti * 128:(ti + 1) * 128], B.QD[:, ti * 128:(ti + 1) * 128], True, True, r=[B.tKD, B.tQD], w=[tpat])
                MK = MASKS if is_s else MASKP
                TT(B.AM[:, :ntile, :], v3(pat[:, :n], ntile), MK.unsqueeze(1).to_broadcast([128, ntile, 128]), ALU.mult, r=[tpat, tCONS], w=[B.tAM])
                pk_, tpk = pbank()
                pkb = pk_[:].bitcast(BF16).rearrange("p (a b) -> p a b", a=8)
                for ti in range(ntile):
                    TRP(pkb[:, ti, :], B.KT[:, ti * 128:(ti + 1) * 128], r=[B.tKT, tCB], w=[tpk])
                ACT(B.KTT[:, :ntile, :], pkb[:, :ntile, :], AF.Copy, r=[tpk], w=[B.tKTT])
                yield
                if not is_s:
                    ACT(B.SBFC[:, 0, :], S32x[0][:, h, :], AF.Copy, r=[tS32h[0][h]], w=[B.tSBFC])
                    if KVMODE == "batch":
                        for c in range(nch):
                            ti, hf = c // 2, c % 2
                            bi = B.kvb[c // 4]
                            sl = PS[bi][:, (c % 4) * 128:(c % 4 + 1) * 128]
                            MM(sl, B.KTT[hf * 64:(hf + 1) * 64, ti, :], B.VT[hf * 64:(hf + 1) * 64, ti, :], True, True, r=[B.tKTT, B.tVT], w=[tPS[bi]])
                        yield
                    def kv_mm(c):
                        ti, hf = c // 2, c % 2
                        bi = 6 + c % 2
                        sl = PS[bi][:, (2 * B.sidx + (c // 2) % 2) * 128:(2 * B.sidx + (c // 2) % 2 + 1) * 128]
                        MM(sl, B.KTT[hf * 64:(hf + 1) * 64, ti, :], B.VT[hf * 64:(hf + 1) * 64, ti, :], True, True, r=[B.tKTT, B.tVT], w=[tPS[bi]])

                    if KVMODE != "batch":
                        kv_mm(0)
                        kv_mm(1)
                    for c in range(nch):
                        if KVMODE == "batch":
                            bi = B.kvb[c // 4]
                            sl = PS[bi][:, (c % 4) * 128:(c % 4 + 1) * 128]
                        else:
                            bi = 6 + c % 2
                            sl = PS[bi][:, (2 * B.sidx + (c // 2) % 2) * 128:(2 * B.sidx + (c // 2) % 2 + 1) * 128]
                        si, so = c % 2, (c + 1) % 2
                        STT(S32x[so][:, h, :], S32x[si][:, h, :], B.P[:, c * 64 + 63:c * 64 + 64], sl, ALU.mult, ALU.add,
                            r=[tS32h[si][h], B.tP, tPS[bi]], w=[tS32h[so][h]])
                        ACT(B.SBFC[:, c + 1, :], S32x[so][:, h, :], AF.Copy, r=[tS32h[so][h]], w=[B.tSBFC])
                        if KVMODE != "batch" and c + 2 < nch:
                            kv_mm(c + 2)
                        yield
                    po, tpo = pbank() if KVMODE == "batch" else (PS[5], tPS[5])
                    for ti in range(ntile):
                        MM(po[:, ti * 128:(ti + 1) * 128], B.VT[:, ti, :], B.AM[:, ti, :], True, False, r=[B.tVT, B.tAM], w=[tpo])
                        for hf in range(2):
                            c = 2 * ti + hf
                            MM(po[:, c * 64:(c + 1) * 64], B.SBFC[:, c, :], B.QD[:, c * 64:(c + 1) * 64], False, hf == 1, r=[B.tSBFC, B.tQD], w=[tpo])
                    ACT(B.OSQ[:, :n], po[:, :n], AF.Square, r=[tpo], w=[B.tOSQ])
                    ACT(B.O32[:, :n], po[:, :n], AF.Identity, r=[tpo, tCO], w=[B.tO32], scale=NW2[:, h:h + 1])
                    yield
                else:
                    K.dma(S0, d_shg[:, h].rearrange("n k v -> k n v"), tS0, True)
                    ACT(S0B, S0, AF.Copy, r=[tS0], w=[tS0B]); yield
                    po, tpo = pbank() if KVMODE == "batch" else (PS[5], tPS[5])
                    MM(po[:, 0:128], B.VT[:, 0, :], B.AM[:, 0, :], True, False, r=[B.tVT, B.tAM], w=[tpo])
                    for sq_ in range(16):
                        MM(po[:, sq_ * 8:(sq_ + 1) * 8], S0B[:, sq_, :], B.QD[:, sq_ * 8:(sq_ + 1) * 8], False, sq_ == 15, r=[tS0B, B.tQD], w=[tpo])
                    ACT(B.O32[:, :n], po[:, :n], AF.Identity, r=[tpo, tCO], w=[B.tO32], scale=NW2[:, h:h + 1])
                    ACT(B.OSQ[:, :n], po[:, :n], AF.Square, r=[tpo], w=[B.tOSQ])
                    yield
                    TT(VEXP, B.VT[:, 0, :].unsqueeze(1).to_broadcast([128, 16, 128]), SEQMB.unsqueeze(2).to_broadcast([128, 16, 128]), ALU.mult, r=[B.tVT, tCB], w=[tVEXP])
                    Ps3 = B.P[:, :128].rearrange("p (a b) -> p a b", a=16)
                    TT(S0, S0, Ps3[:, :, 7:8].to_broadcast([128, 16, 128]), ALU.mult, r=[tS0, B.tP], w=[tS0], eng="pool")
                    yield
                    for qd_ in range(4):
                        pbk, tbk = pbank()
                        MM(pbk[:, :], B.KTT[:, 0, :], VEXP[:, qd_ * 4:(qd_ + 1) * 4, :], True, True, r=[B.tKTT, tVEXP], w=[tbk])
                        TT(S0[:, qd_ * 4:(qd_ + 1) * 4, :], S0[:, qd_ * 4:(qd_ + 1) * 4, :], v3(pbk[:, :], 4), ALU.add, r=[tS0, tbk], w=[tS0])
                        yield
                    K.dma(o_nhs[:, h].rearrange("n k v -> k n v"), S0, tS0, False)
                pss, tpss = pbank()
                for ti in range(ntile):
                    MM(pss[:, ti:ti + 1], B.OSQ[:, ti * 128:(ti + 1) * 128], ONESB[:, 0:1], True, True, r=[B.tOSQ, tCB], w=[tpss])
                xs_, ys_, ts_, hs_ = [B.RSs[:, i * 4:i * 4 + ntile] for i in range(4)]
                TS(xs_, pss[:, 0:ntile], 128.0 * RMS_EPS, None, ALU.add, r=[tpss], w=[B.tRSs]); yield
                TS(ys_.bitcast(I32), xs_.bitcast(I32), 1, None, ALU.arith_shift_right, r=[B.tRSs], w=[B.tRSs])
                TS(ys_.bitcast(I32), ys_.bitcast(I32), -1, 0x5f3759df, ALU.mult, ALU.add, r=[B.tRSs], w=[B.tRSs])
                TS(hs_, xs_, -0.5, None, ALU.mult, r=[B.tRSs], w=[B.tRSs]); yield
                for _ in range(3):
                    TT(ts_, ys_, ys_, ALU.mult, r=[B.tRSs], w=[B.tRSs])
                    TT(ts_, ts_, hs_, ALU.mult, r=[B.tRSs], w=[B.tRSs])
                    STT(ys_, ts_, 1.5, ys_, ALU.add, ALU.mult, r=[B.tRSs], w=[B.tRSs]); yield
                bmv = B.BM[:, :, :].rearrange("p a b -> p (a b)").bitcast(BF16)
                nn = ntile * 128
                bmh = bmv[:, 0:nn].rearrange("p (a b) -> p a b", a=ntile)
                bml = bmv[:, nn:2 * nn].rearrange("p (a b) -> p a b", a=ntile)
                hi_b = B.RSs[:, 8:8 + 2].bitcast(BF16)[:, 0:ntile]
                CPY(hi_b, ys_, r=[B.tRSs], w=[B.tRSs])
                TT(hs_, ys_, hi_b, ALU.subtract, r=[B.tRSs], w=[B.tRSs])
                TT(bmh, IDN32.unsqueeze(1).to_broadcast([128, ntile, 128]), hi_b.unsqueeze(2).to_broadcast([128, ntile, 128]), ALU.mult,
                   r=[tCONS, B.tRSs], w=[B.tBM], eng="pool")
                TT(bml, IDN32.unsqueeze(1).to_broadcast([128, ntile, 128]), hs_.unsqueeze(2).to_broadcast([128, ntile, 128]), ALU.mult,
                   r=[tCONS, B.tRSs], w=[B.tBM2])
                yield
                pbc, tpbc = pbank()
                MM(pbc[:, :n], ONESB, bmv[:, 0:nn], True, False, r=[tCB, B.tBM], w=[tpbc])
                MM(pbc[:, :n], ONESB, bmv[:, nn:2 * nn], False, True, r=[tCB, B.tBM2], w=[tpbc])
                TT(B.O32[:, :n], B.O32[:, :n], pbc[:, :n], ALU.mult, r=[B.tO32, tpbc], w=[B.tO32]); yield
                TT(YHG[:, h, c0:c0 + n], B.O32[:, :n], B.SG[:, :n], ALU.mult, r=[B.tO32, B.tSG], w=[tYHG])
                yield

            def get_units(h):
                qf_, tqf = next_unit()
                vo_, tvo = next_unit()
                units[h] = (v3(qf_, 8), tqf, v3(vo_, 8), tvo)

            def chain(*gs):
                for g_ in gs:
                    yield from g_

            def zip_run(gens):
                gens = list(gens)
                if not ZIP:
                    for g_ in gens:
                        for _ in g_:
                            pass
                    return
                while gens:
                    for g_ in list(gens):
                        try:
                            next(g_)
                        except StopIteration:
                            gens.remove(g_)

            sbusy = [False]

            def stream_chain(heads, B):
                for h_ in heads:
                    get_units(h_)
                    for (c0_, n_, _) in ptgs:
                        yield from head_stream(h_, B, c0_, n_, False)
                    if last:
                        while sbusy[0]:
                            yield
                        sbusy[0] = True
                        yield from head_stream(h_, B, plen, 128, True)
                        sbusy[0] = False

            gA = stream_chain([0, 2, 4, 6], HS[0])
            gB = stream_chain([1, 3, 5, 7], HS[1])
            for _ in range(STAGGER):
                next(gA)
            while pending_out:
                dst_, src_ = pending_out.pop(0)
                K.dma(dst_, src_, tX1, False)
            zip_run([gA, gB])
            if last:
                for h in range(8):
                    K.dma(o_nhp[h], S32[:, h, :], tS32h[0][h], False)
            fence(hg_toks + pl_toks)
            pnb[0] = 6
            if ps_ == 0:
                mod_units(8, 24)
                TS(MODT[:, 32:40, :], MODT[:, 32:40, :], 1.0, None, ALU.add, r=[tMODT], w=[tMODT])
                TS(MODT[:, 16:24, :], MODT[:, 16:24, :], 0.5 / ALPHA, None, ALU.mult, r=[tMODT], w=[tMODT])
                TS(MODT[:, 40:48, :], MODT[:, 40:48, :], 1.0 / ALPHA, None, ALU.mult, r=[tMODT], w=[tMODT])

            for g in range(4):
                if g % 2 == 0:
                    pw_, tpw = next_unit(); pw = v3(pw_, 8)
                w = 2 << g
                gc = (g % 2) * 128
                for (c0_, n_, _) in ptgs:
                    pz, tpz = proj(pw, tpw, 8, gc, UT, tUT, c0_, n_)
                    CPY(ZP[:, 0:16], HIST[:, g, :], r=[tHIST], w=[tZP])
                    ACT(ZP[:, 16:16 + n_], pz[:, :n_], AF.Copy, r=[tpz], w=[tZP])
                    CPY(HIST[:, g, :], ZP[:, n_:n_ + 16], r=[tZP], w=[tHIST])
                    cur, tcur = ZP, tZP
                    for j in range(g + 1):
                        sh = 1 << j
                        nx, tnx = (PA, tPA) if j % 2 == 0 else (PB, tPB)
                        TT(nx[:, sh:16 + n_], cur[:, sh:16 + n_], cur[:, 0:16 + n_ - sh], ALU.add, r=[tcur], w=[tnx])
                        cur, tcur = nx, tnx
                    STT(MT[:, c0_:c0_ + n_], cur[:, 16:16 + n_], 1.0 / w, ZP[:, 16:16 + n_], ALU.mult, ALU.subtract, r=[tcur, tZP], w=[tMT])
                    if ps_ == 0 and c0_ == 0:
                        TT(T16, cur[:, 16:32], INVC[:, g, :], ALU.mult, r=[tcur, tCONS], w=[tT16])
                        TT(MT[:, 0:16], T16, ZP[:, 16:32], ALU.subtract, r=[tT16, tZP, tMT], w=[tMT])
                    if last and c0_ + n_ == plen:
                        K.dma(o_zpp[:, g, :], ZP[:, n_ + 1:n_ + 16], tZP, False)
                if last:
                    MSET(ZPS, 0.0, w=[tZPS])
                    K.dma(ZPS[:, :, 1:16], d_spT[:, g], tZPS, True)
                    pzs, tpzs = proj(pw, tpw, 8, gc, UT, tUT, plen, 128)
                    ACT(ZPS[:, :, 16:24], s3(pzs[:, 0:128]), AF.Copy, r=[tpzs], w=[tZPS])
                    cur, tcur = ZPS, tZPS
                    for j in range(g + 1):
                        sh = 1 << j
                        nx, tnx = (PAS, tPAS) if j % 2 == 0 else (PBS, tPBS)
                        TT(nx[:, :, sh:24], cur[:, :, sh:24], cur[:, :, 0:24 - sh], ALU.add, r=[tcur], w=[tnx])
                        cur, tcur = nx, tnx
                    STT(s3(MT[:, plen:plen + 128]), cur[:, :, 16:24], 1.0 / w, ZPS[:, :, 16:24], ALU.mult, ALU.subtract, r=[tcur, tZPS, tMT], w=[tMT])
                    K.dma(o_zps[:, g], ZPS[:, :, 9:24], tZPS, False)
                for (c0, n, is_s) in tgs:
                    pb, pt = pbank()
                    MM(pb[:, :n], WPG[:, g, :], MT[:, c0:c0 + n], True, True, r=[tWPG, tMT], w=[pt])
                    ACT(YPOOL[:, g, c0:c0 + n], pb[:, :n], AF.Identity, r=[pt, tSMALL], w=[tYPOOL], scale=PSC[:, g:g + 1])
            fence(pl_toks + ln_toks)

            for u in range(4):
                wa_, twa = next_unit(); wa = v3(wa_[:, :1024], 4)
                ga_, tga = next_unit(); ga = v3(ga_, 8)
                wb_, twb = next_unit(); wbv = v3(wb_, 8)
                gb_, tgb = next_unit(); gb = v3(gb_, 8)
                for jj in range(2):
                    j = 2 * u + jj
                    for (c0, n, is_s) in tgs:
                        pa, tpa = proj(wa, twa, 4, jj * 128, YPOOL, tYPOOL, c0, n)
                        pg, tpg = proj(ga, tga, 8, jj * 128, UT, tUT, c0, n)
                        mctr[0] += 1
                        ia, ib = (mctr[0] % 2) * 2, (mctr[0] % 2) * 2 + 1
                        ACT(TA[ia][:, :n], pg[:, :n], AF.Tanh, r=[tpg], w=[tTA[ia]], scale=0.5)
                        STT(TM[0][:, :n], TA[ia][:, :n], 1.0, pa[:, :n], ALU.add, ALU.mult, r=[tTA[ia], tpa], w=[tTM[0]])
                        pb2, tpb2 = proj(wbv, twb, 8, jj * 128, YHG, tYHG, c0, n)
                        pg2, tpg2 = proj(gb, tgb, 8, jj * 128, UT, tUT, c0, n)
                        ACT(TA[ib][:, :n], pg2[:, :n], AF.Tanh, r=[tpg2], w=[tTA[ib]], scale=0.5)
                        STT(TM[1][:, :n], TA[ib][:, :n], 1.0, pb2[:, :n], ALU.add, ALU.mult, r=[tTA[ib], tpb2], w=[tTM[1]])
                        TT(MRG[:, j, c0:c0 + n], TM[0][:, :n], TM[1][:, :n], ALU.add, r=[tTM[0], tTM[1]], w=[tMRG], eng="pool")

            for u in range(4):
                wo_, two = next_unit(); wo = v3(wo_, 8)
                for jj in range(2):
                    j = 2 * u + jj
                    xs_j, txs_j = load_x_chunk(j, ps_)
                    for (c0, n, is_s) in tgs:
                        pm, tpm = proj(wo, two, 8, jj * 128, MRG, tMRG, c0, n)
                        if not is_s:
                            STT(X1[:, j, c0:c0 + n], pm[:, :n], mcol(16 + j), xs_j[:, c0:c0 + n], ALU.mult, ALU.add, r=[tpm, tMODT, txs_j], w=[tX1])
                        else:
                            TT(s3(TM[0][:, 0:128]), s3(pm[:, 0:128]), mrow(16 + j), ALU.mult, r=[tpm, tMODT], w=[tTM[0]])
                            TT(X1[:, j, c0:c0 + 128], TM[0][:, 0:128], xs_j[:, c0:c0 + 128], ALU.add, r=[tTM[0], txs_j], w=[tX1])
            fence([tUT] + tHT_all)
            for (c0, n, is_s) in tgs:
                layer_norm_inplace(c0, n)
                for k in range(8):
                    TS(X1[:, k, c0:c0 + n], X1[:, k, c0:c0 + n], LG1[:, k:k + 1], LB1[:, k:k + 1], ALU.mult, ALU.add, r=[tX1, tSMALL], w=[tX1])
                    if not is_s:
                        ACT(HT[:, k, c0:c0 + n], X1[:, k, c0:c0 + n], AF.Identity, r=[tX1, tMODT], w=[tHTg[c0]], scale=mcol(32 + k), bias=mcol(24 + k))
                    else:
                        TT(s3(TM[0][:, 0:128]), s3(X1[:, k, c0:c0 + 128]), mrow(32 + k), ALU.mult, r=[tX1, tMODT], w=[tTM[0]])
                        TT(s3(HT[:, k, c0:c0 + 128]), s3(TM[0][:, 0:128]), mrow(24 + k), ALU.add, r=[tTM[0], tMODT], w=[tHTg[c0]])
            fence([tYHG, tMRG, tYPOOL, tAT])

            for u in range(11):
                wg_, twg = next_unit(); wg = v3(wg_, 8)
                wu_, twu = next_unit(); wu = v3(wu_, 8)
                for jj in range(2):
                    fc = 2 * u + jj
                    for (c0, n, is_s) in tgs:
                        pg, tpg = proj(wg, twg, 8, jj * 128, HT, tHTg[c0], c0, n)
                        pu, tpu = proj(wu, twu, 8, jj * 128, HT, tHTg[c0], c0, n)
                        mctr[0] += 1
                        i2 = mctr[0] % 4
                        ACT(TA[i2][:, :n], pg[:, :n], AF.Silu, r=[tpg], w=[tTA[i2]])
                        TT(AT[:, fc, c0:c0 + n], TA[i2][:, :n], pu[:, :n], ALU.mult, r=[tTA[i2], tpu], w=[tAT])
            if not last:
                phase1(ps_ + 1)
            for j in range(8):
                wda_, twda = next_unit(); wda = v3(wda_[:, :1408], 11)
                wdb_, twdb = next_unit(); wdb = v3(wdb_[:, :1408], 11)
                for (c0, n, is_s) in tgs:
                    pd, tpd = pbank()
                    for kc in range(22):
                        wv, wt = (wda, twda) if kc < 11 else (wdb, twdb)
                        MM(pd[:, :n], wv[:, kc % 11, :], AT[:, kc, c0:c0 + n], kc == 0, kc == 21, r=[wt, tAT], w=[tpd])
                    if not is_s:
                        STT(X1[:, j, c0:c0 + n], pd[:, :n], mcol(40 + j), X1[:, j, c0:c0 + n], ALU.mult, ALU.add, r=[tpd, tMODT, tX1], w=[tX1])
                    else:
                        TT(s3(TM[0][:, 0:128]), s3(pd[:, 0:128]), mrow(40 + j), ALU.mult, r=[tpd, tMODT], w=[tTM[0]])
                        TT(X1[:, j, c0:c0 + 128], TM[0][:, 0:128], X1[:, j, c0:c0 + 128], ALU.add, r=[tTM[0], tX1], w=[tX1])
            for (c0, n, is_s) in tgs:
                layer_norm_inplace(c0, n)
                for k in range(8):
                    TS(X1[:, k, c0:c0 + n], X1[:, k, c0:c0 + n], LG2[:, k:k + 1], LB2[:, k:k + 1], ALU.mult, ALU.add, r=[tX1, tSMALL], w=[tX1])
                dcol = 2048 if is_s else base + c0
                if last:
                    K.dma(o_yT[:, :, dcol:dcol + n], X1[:, :, c0:c0 + n], tX1, False)
                else:
                    pending_out.append((o_yT[:, :, dcol:dcol + n], X1[:, :, c0:c0 + n]))

        final = [tX1, tZP, tZPS, tS0] + tS32h[0]
        K.emit(es, final_toks=[t for t in final if t.dsem is not None])
    return nc


def _units(w, kc, ncol):
    Kd, N = w.shape
    assert Kd == kc * 128
    nu = N // ncol
    return np.ascontiguousarray(w.reshape(kc, 128, nu, ncol).transpose(2, 1, 0, 3).reshape(nu, 128, kc * ncol))


_NC_CACHE = {}


def prep_inputs(x_prompt, x_sample, state_pool, state_hgrn, c_prompt, c_sample,
                w_ada, b_ada, w_in, w_pool_grp, pool_scale, lb_logits, hgrn_norm_w,
                w_a, w_b, w_out, ln1_g, ln1_b, w_gate, w_up, w_down, ln2_g, ln2_b, cores=None):
    f = lambda a: np.asarray(a, dtype=np.float32)
    x_prompt, x_sample, state_pool, state_hgrn = f(x_prompt), f(x_sample), f(state_pool), f(state_hgrn)
    c_prompt, c_sample = f(c_prompt), f(c_sample)
    w_ada, b_ada, w_in = f(w_ada)[0], f(b_ada)[0], f(w_in)[0]
    w_pool_grp, pool_scale, lb_logits, hgrn_norm_w = f(w_pool_grp)[0], f(pool_scale)[0], f(lb_logits), f(hgrn_norm_w)[0]
    w_a, w_b, w_out = f(w_a)[0], f(w_b)[0], f(w_out)[0]
    w_gate, w_up, w_down = f(w_gate)[0], f(w_up)[0], f(w_down)[0]
    ln1_g, ln1_b, ln2_g, ln2_b = f(ln1_g)[0], f(ln1_b)[0], f(ln2_g)[0], f(ln2_b)[0]

    colT = lambda v, nk: np.ascontiguousarray(v.reshape(nk, 128).T)
    wada_u = _units(w_ada, 8, 256)
    Pq, Pf, Pi, Pog, Pga, Pgb = 512, 1536, 2560, 3584, 4608, 5632
    cols = list(range(0, 512))
    for h in range(8):
        cols += list(range(Pq + h * 128, Pq + (h + 1) * 128)) + list(range(Pf + h * 128, Pf + (h + 1) * 128))
        cols += list(range(Pi + h * 128, Pi + (h + 1) * 128)) + list(range(Pog + h * 128, Pog + (h + 1) * 128))
    cols += list(range(Pga, Pga + 1024)) + list(range(Pgb, Pgb + 1024))
    win_u = _units(w_in[:, np.array(cols)], 8, 256)
    wpg = np.ascontiguousarray(w_pool_grp.transpose(1, 0, 2).reshape(128, 512))
    wa_u = _units(w_a, 4, 256)
    wb_u = _units(w_b, 8, 256)
    wo_u = _units(w_out, 8, 256)
    wg_u = _units(w_gate, 8, 256)
    wu_u = _units(w_up, 8, 256)
    wd4 = w_down.reshape(2, 11, 128, 8, 128)
    wd_u = np.ascontiguousarray(wd4.transpose(3, 0, 2, 1, 4).reshape(16, 128, 1408))
    small = np.zeros((128, 64), np.float32)
    small[:, 0:4] = colT(pool_scale, 4)
    small[:, 4:12] = colT(lb_logits[0], 8)
    small[:, 12:20] = colT(lb_logits[1], 8)
    small[:, 20:28] = colT(hgrn_norm_w, 8)
    small[:, 28:36] = colT(ln1_g, 8)
    small[:, 36:44] = colT(ln1_b, 8)
    small[:, 44:52] = colT(ln2_g, 8)
    small[:, 52:60] = colT(ln2_b, 8)
    badaT = colT(b_ada, 48)
    consts = np.zeros((128, 464), np.float32)
    consts[:, 0:128] = np.eye(128, dtype=np.float32)
    s_i = np.arange(128)[:, None]
    t_i = np.arange(128)[None, :]
    consts[:, 128:256] = ((s_i // 64 == t_i // 64) & (s_i <= t_i)).astype(np.float32)
    consts[:, 256:384] = ((s_i // 8 == t_i // 8) & (s_i <= t_i)).astype(np.float32)
    consts[:, 384:400] = (s_i // 8 == np.arange(16)[None, :]).astype(np.float32)
    for g in range(4):
        wdw = 2 << g
        consts[:, 400 + g * 16:416 + g * 16] = (1.0 / np.minimum(np.arange(16) + 1, wdw)).astype(np.float32)[None, :]

    in_maps = []
    for c in (range(NCORE) if cores is None else cores):
        xs = x_sample[c * 16:(c + 1) * 16].reshape(128, 1024)
        xall = np.concatenate([x_prompt[c], xs], axis=0)
        xT = np.ascontiguousarray(xall.T.reshape(8, 128, 2176).transpose(1, 0, 2))
        call = np.concatenate([c_prompt[c:c + 1], c_sample[c * 16:(c + 1) * 16]], axis=0)
        cT = np.ascontiguousarray(call.T.reshape(8, 128, 17).transpose(1, 0, 2))
        sp = state_pool[0, c * 16:(c + 1) * 16]
        spT = np.ascontiguousarray(sp.reshape(16, 15, 4, 128).transpose(3, 2, 0, 1))
        shg = np.ascontiguousarray(state_hgrn[0, c * 16:(c + 1) * 16])
        in_maps.append({"xT": xT, "cT": cT, "badaT": badaT, "wada": wada_u, "win": win_u, "wpg": wpg, "wa": wa_u, "wb": wb_u,
                        "wo": wo_u, "wg": wg_u, "wu": wu_u, "wd": wd_u, "small": small, "consts": consts, "spT": spT, "shg": shg})
    return in_maps


def assemble_core(r):
    yT = np.asarray(r["yT"], dtype=np.float32)
    yall = yT.transpose(2, 1, 0).reshape(2176, 1024)
    y_p = yall[:2048]
    y_s = yall[2048:].reshape(16, 8, 1024)
    npp = np.asarray(r["zpp"], dtype=np.float32).transpose(2, 1, 0).reshape(15, 512)
    nps = np.asarray(r["zps"], dtype=np.float32).transpose(2, 3, 1, 0).reshape(16, 15, 512)
    return y_p, y_s, npp, np.asarray(r["nhp"], dtype=np.float32), nps, np.asarray(r["nhs"], dtype=np.float32)


def kernel(**inputs):
    in_maps = prep_inputs(**inputs)
    if "nc" not in _NC_CACHE:
        _NC_CACHE["nc"] = build_nc()
    res = run_bass_kernel_spmd(_NC_CACHE["nc"], in_maps, core_ids=list(range(NCORE)))
    R = res.results
    y_p = np.zeros((8, 2048, 1024), np.float32)
    y_s = np.zeros((128, 8, 1024), np.float32)
    npp = np.zeros((1, 8, 15, 512), np.float32)
    nhp = np.zeros((1, 8, 8, 128, 128), np.float32)
    nps = np.zeros((1, 128, 15, 512), np.float32)
    nhs = np.zeros((1, 128, 8, 128, 128), np.float32)
    for c in range(NCORE):
        a, b, d, e, g, h = assemble_core(R[c])
        y_p[c] = a
        y_s[c * 16:(c + 1) * 16] = b
        npp[0, c] = d
        nhp[0, c] = e
        nps[0, c * 16:(c + 1) * 16] = g
        nhs[0, c * 16:(c + 1) * 16] = h
    return (y_p, y_s, npp, nhp, nps, nhs)
```

```python
import numpy as np
from contextlib import ExitStack
import concourse.bass as bass
import concourse.mybir as mybir
from concourse.bass_utils import run_bass_kernel_spmd

F32 = mybir.dt.float32
BF16 = mybir.dt.bfloat16
AF = mybir.ActivationFunctionType
ALU = mybir.AluOpType

ALPHA = 2.0 ** 0.25
LN_EPS = 1e-5
RMS_EPS = 1e-6
NCORE = 8
TL = 768
NPASS = 3
PASSES = [(0, 768), (768, 768), (1536, 512)]
RSTD_MODE = "tokb"
ZIP = True
KVMODE = "alt"
USE_SCR = True
STAGGER = 32


class Tok:
    __slots__ = ("name", "lw", "rd", "dsem", "dcount")

    def __init__(self, name):
        self.name = name
        self.lw = None
        self.rd = {}
        self.dsem = None
        self.dcount = 0


class Op:
    __slots__ = ("eng", "fn", "deps", "inc", "val", "dtok", "dval")

    def __init__(self, eng, fn):
        self.eng = eng
        self.fn = fn
        self.deps = []
        self.inc = False
        self.val = 0
        self.dtok = None
        self.dval = 0


class Trk:
    ENGS = ("pe", "act", "dve", "pool", "sp")

    def __init__(self, nc):
        self.nc = nc
        self.ops = {e: [] for e in self.ENGS}
        self.dtoks = []

    def tok(self, name="t"):
        return Tok(name)

    def _collect(self, eng, r, w):
        deps = {}

        def add(ref, raw):
            if ref is None:
                return
            if ref[0] == "e":
                o2 = ref[1]
                if o2.eng == eng and eng == "pe":
                    return
                deps[id(o2)] = ref
            else:
                key = ("d", id(ref[1]))
                if key not in deps or deps[key][2] < ref[2]:
                    deps[key] = ref

        for t in r:
            add(t.lw, True)
        for t in w:
            add(t.lw, False)
            for x in t.rd.values():
                add(x, False)
        return list(deps.values())

    def op(self, eng, fn, r=(), w=()):
        o = Op(eng, fn)
        o.deps = self._collect(eng, r, w)
        for d in o.deps:
            if d[0] == "e":
                d[1].inc = True
        ref = ("e", o)
        for t in w:
            t.lw = ref
            t.rd = {}
        for t in r:
            t.rd[eng] = ref
        self.ops[eng].append(o)
        return o

    def dma(self, out, in_, tok, write, r=(), w=()):
        q = "sp"
        rr = list(r) + ([] if write else [tok])
        ww = list(w) + ([tok] if write else [])
        o = Op(q, lambda e: e.dma_start(out=out, in_=in_))
        o.deps = self._collect(q, rr, ww)
        for d in o.deps:
            if d[0] == "e":
                d[1].inc = True
        if tok.dsem is None:
            tok.dsem = "pending"
            self.dtoks.append(tok)
        tok.dcount += 16
        o.dtok = tok
        o.dval = tok.dcount
        ref = ("d", tok, tok.dcount)
        for t in ww:
            t.lw = ref
            t.rd = {}
        for t in rr:
            t.rd[("d", id(tok))] = ref
        self.ops[q].append(o)
        return o

    def emit(self, es, final_toks=()):
        nc = self.nc
        esem = {e: es.enter_context(nc.semaphore("s_" + e)) for e in self.ENGS}
        for i, t in enumerate(self.dtoks):
            t.dsem = es.enter_context(nc.semaphore("d%d" % i))
        for e in self.ENGS:
            c = 0
            for o in self.ops[e]:
                if o.inc:
                    c += 1
                    o.val = c
        fin = Op("sp", None)
        fin.deps = [("d", t, t.dcount) for t in final_toks]
        self.ops["sp"].append(fin)
        block = es.enter_context(nc.Block())
        ops = self.ops

        def run(e, name):
            seen = {}
            for o in ops[name]:
                for d in o.deps:
                    if d[0] == "e":
                        sem, val, key = esem[d[1].eng], d[1].val, d[1].eng
                    else:
                        sem, val, key = d[1].dsem, d[2], id(d[1])
                    if seen.get(key, 0) >= val:
                        continue
                    e.wait_ge(sem, val)
                    seen[key] = val
                if o.fn is None:
                    continue
                ins = o.fn(e)
                if o.dtok is not None:
                    ins.then_inc(o.dtok.dsem, 16)
                elif o.inc:
                    ins.then_inc(esem[name], 1)

        @block.tensor
        def _(e):
            run(e, "pe")

        @block.scalar
        def _(e):
            run(e, "act")

        @block.vector
        def _(e):
            run(e, "dve")

        @block.gpsimd
        def _(e):
            run(e, "pool")

        @block.sync
        def _(e):
            run(e, "sp")


def build_nc():
    nc = bass.Bass("TRN2", target_bir_lowering=False)
    din = lambda n, s: nc.dram_tensor(n, list(s), F32, kind="ExternalInput").ap()
    dout = lambda n, s: nc.dram_tensor(n, list(s), F32, kind="ExternalOutput").ap()
    d_xT = din("xT", [128, 8, 2176])
    d_cT = din("cT", [128, 8, 17])
    d_badaT = din("badaT", [128, 48])
    d_wada = din("wada", [24, 128, 2048])
    d_win = din("win", [26, 128, 2048])
    d_wpg = din("wpg", [128, 512])
    d_wa = din("wa", [4, 128, 1024])
    d_wb = din("wb", [4, 128, 2048])
    d_wo = din("wo", [4, 128, 2048])
    d_wg = din("wg", [11, 128, 2048])
    d_wu = din("wu", [11, 128, 2048])
    d_wd = din("wd", [16, 128, 1408])
    d_small = din("small", [128, 64])
    d_consts = din("consts", [128, 464])
    d_spT = din("spT", [128, 4, 16, 15])
    d_shg = din("shg", [16, 8, 128, 128])
    o_yT = dout("yT", [128, 8, 2176])
    o_zpp = dout("zpp", [128, 4, 15])
    o_zps = dout("zps", [128, 4, 16, 15])
    o_nhp = dout("nhp", [8, 128, 128])
    o_nhs = dout("nhs", [16, 8, 128, 128])

    NSCR = 76
    d_scr = nc.dram_tensor("wscr", [NSCR, 128, 2048], BF16, kind="Internal").ap()

    es = ExitStack()
    with es:
        K = Trk(nc)
        T = K.tok
        sb = lambda n, s, d=F32: es.enter_context(nc.sbuf_tensor(n, list(s), d))
        ARENA_F = 53200
        arena = sb("arena", [128, ARENA_F], F32)
        apos = [0]

        def carve_at(off, nel, dt):
            nb = nel * (2 if dt == BF16 else 4)
            assert off % 4 == 0
            a = arena[:, off // 4:(off + nb + 3) // 4]
            if dt == BF16:
                a = a.bitcast(BF16)[:, :nel]
            return a

        def carve(nel, dt):
            off = apos[0]
            nb = (nel * (2 if dt == BF16 else 4) + 63) // 64 * 64
            apos[0] += nb
            assert apos[0] <= ARENA_F * 4, "arena overflow"
            return carve_at(off, nel, dt)

        v3 = lambda a, n0: a.rearrange("p (a b) -> p a b", a=n0)
        UT = v3(carve(8 * TL, BF16), 8); tUT = T("UT")
        HT = UT
        tHTg = {c_: T("HT%d" % c_) for c_ in (0, 512)}
        tHT_all = list(tHTg.values())
        X1 = v3(carve(8 * TL, F32), 8); tX1 = T("X1")
        XS = [carve(TL, F32) for _ in range(3)]; tXS = [T("XS%d" % i) for i in range(3)]
        at_off = apos[0]
        AT = v3(carve(22 * TL, BF16), 22); tAT = T("AT")
        apos[0] = at_off
        YHG = v3(carve(8 * TL, BF16), 8); tYHG = T("YHG")
        MRG = v3(carve(8 * TL, BF16), 8); tMRG = T("MRG")
        YPOOL = v3(carve(4 * TL, BF16), 4); tYPOOL = T("YPOOL")
        apos[0] = max(apos[0], at_off + 22 * TL * 2)
        rc_off = apos[0]
        class HSet:
            pass

        def mk_set(wd, nt, nm):
            S_ = HSet()
            S_.Fa = carve(wd, F32); S_.tFa = T(nm + "Fa")
            S_.Fk = carve(wd, F32); S_.tFk = T(nm + "Fk")
            S_.P = carve(wd, F32); S_.tP = T(nm + "P")
            S_.rP = carve(wd, F32); S_.trP = T(nm + "rP")
            S_.Qs = carve(wd, F32); S_.tQs = T(nm + "Qs")
            S_.KD = carve(wd, BF16); S_.tKD = T(nm + "KD")
            S_.KT = carve(wd, BF16); S_.tKT = T(nm + "KT")
            S_.QD = carve(wd, BF16); S_.tQD = T(nm + "QD")
            S_.VT = v3(carve(wd, BF16), nt); S_.tVT = T(nm + "VT")
            S_.SG = carve(wd, F32); S_.tSG = T(nm + "SG")
            S_.AM = v3(carve(wd, BF16), nt); S_.tAM = T(nm + "AM")
            S_.KTT = v3(carve(wd, BF16), nt); S_.tKTT = T(nm + "KTT")
            S_.O32 = S_.rP; S_.tO32 = S_.trP
            S_.OSQ = S_.KT; S_.tOSQ = S_.tKT
            S_.BM = v3(S_.Qs, nt); S_.tBM = S_.tQs
            S_.tBM2 = T(nm + "BM2")
            S_.RSs = carve(16, F32); S_.tRSs = T(nm + "RSs")
            S_.D1 = carve(wd, F32); S_.tD1 = T(nm + "D1")
            S_.toks = [S_.tFa, S_.tFk, S_.tP, S_.trP, S_.tQs, S_.tKD, S_.tKT, S_.tQD, S_.tVT, S_.tSG, S_.tAM, S_.tKTT, S_.tRSs, S_.tD1]
            return S_

        HS = [mk_set(512, 4, "A"), mk_set(512, 4, "B")]
        SS = mk_set(128, 1, "S")
        for i_, S_ in enumerate(HS):
            S_.SBFC = v3(carve(9 * 128, BF16), 9); S_.tSBFC = T("SBFC%d" % i_)
            S_.kvb = (4 + 2 * i_, 5 + 2 * i_)
            S_.sidx = i_
            S_.toks.append(S_.tSBFC)
        S0 = v3(carve(16 * 128, F32), 16); tS0 = T("S0")
        S0B = v3(carve(16 * 128, BF16), 16); tS0B = T("S0B")
        VEXP = v3(carve(16 * 128, BF16), 16); tVEXP = T("VEXP")
        hg_toks = HS[0].toks + HS[1].toks + SS.toks + [tS0, tS0B, tVEXP]
        rc_end = apos[0]
        apos[0] = rc_off
        ZP = carve(528, F32); tZP = T("ZP")
        PA = carve(528, F32); tPA = T("PA")
        PB = carve(528, F32); tPB = T("PB")
        ZPS = v3(carve(16 * 24, F32), 16); tZPS = T("ZPS")
        PAS = v3(carve(16 * 24, F32), 16); tPAS = T("PAS")
        PBS = v3(carve(16 * 24, F32), 16); tPBS = T("PBS")
        MT = carve(TL, BF16); tMT = T("MT")
        T16 = carve(16, F32); tT16 = T("T16")
        pl_toks = [tZP, tPA, tPB, tZPS, tPAS, tPBS, tMT, tT16]
        rc_end = max(rc_end, apos[0])
        apos[0] = rc_off
        RB = v3(carve(8 * 512, BF16), 8); tRB = T("RB")
        SQ = v3(carve(8 * 512, BF16), 8); tSQ = T("SQ")
        ST = [carve(512, F32) for _ in range(4)]; tST = [T("ST%d" % i) for i in range(4)]
        TA = [carve(512, F32) for _ in range(4)]; tTA = [T("TA%d" % i) for i in range(4)]
        TM = [carve(512, F32) for _ in range(2)]; tTM = [T("TM%d" % i) for i in range(2)]
        tRBk = [T("RB%d" % i) for i in range(8)]; tSQk = [T("SQ%d" % i) for i in range(8)]
        ln_toks = tRBk + tSQk + tST + tTA + tTM
        rc_end = max(rc_end, apos[0])
        apos[0] = rc_end
        STG = [carve(2048, F32) for _ in range(2)]; tSTG = [T("STG%d" % i) for i in range(2)]
        WBF = [carve(2048, BF16) for _ in range(6)]; tWBF = [T("WBF%d" % i) for i in range(6)]
        CONS = carve(464, F32); tCONS = T("CONS")
        SMALL = carve(64, F32); tSMALL = T("SMALL")
        IDB = carve(128, BF16); ONESB = carve(128, BF16); tCB = T("CB")
        SEQMB = carve(16, BF16)
        NHALF = carve(8, F32); ZERO = carve(64, F32); DUMMY = carve(8, F32); tDUM = T("DUM")
        ONES32 = carve(128, F32)
        CTF = v3(carve(8 * 17, F32), 8); tCTF = T("CTF")
        SCT = v3(carve(8 * 17, BF16), 8); tSCT = T("SCT")
        BADAT = carve(48, F32); tBADAT = T("BADAT")
        MODT = v3(carve(48 * 17, F32), 48); tMODT = T("MODT")
        CO = carve(40, F32); tCO = T("CO")
        WPGS = carve(512, F32); tWPGS = T("WPGS")
        WPG = v3(carve(512, BF16), 4); tWPG = T("WPG")
        HIST = v3(carve(4 * 16, F32), 4); tHIST = T("HIST")
        S32 = v3(carve(8 * 128, F32), 8); tS32 = T("S32")
        S32b = v3(carve(8 * 128, F32), 8)
        S32x = [S32, S32b]
        tS32h = [[T("S32_%d_%d" % (b_, h_)) for h_ in range(8)] for b_ in range(2)]
        print("arena bytes used per partition:", apos[0])

        MASKP = CONS[:, 128:256]
        MASKS = CONS[:, 256:384]
        INVC = v3(CONS[:, 400:464], 4)
        PSC = SMALL[:, 0:4]
        LG1, LB1, LG2, LB2 = SMALL[:, 28:36], SMALL[:, 36:44], SMALL[:, 44:52], SMALL[:, 52:60]
        C0, C1, NC1, NW2 = CO[:, 0:8], CO[:, 8:16], CO[:, 16:24], CO[:, 24:32]

        PS = [es.enter_context(nc.psum_tensor("PS%d" % i, [128, 512], F32)) for i in range(8)]
        tPS = [T("PS%d" % i) for i in range(8)]
        prot = [0]

        pnb = [4]

        def pbank():
            i = prot[0] % pnb[0]
            prot[0] += 1
            return PS[i], tPS[i]

        def fence(toks):
            K.op("dve", lambda e: e.memset(DUMMY[:, 0:1], 0.0), (), list(toks) + [tDUM])

        def TS(out, in0, s1, s2, op0, op1=None, r=(), w=(), eng="dve"):
            if op1 is None:
                K.op(eng, lambda e: e.tensor_scalar(out=out, in0=in0, scalar1=s1, scalar2=None, op0=op0), r, w)
            else:
                K.op(eng, lambda e: e.tensor_scalar(out=out, in0=in0, scalar1=s1, scalar2=s2, op0=op0, op1=op1), r, w)

        def TT(out, in0, in1, op, r=(), w=(), eng="dve"):
            K.op(eng, lambda e: e.tensor_tensor(out=out, in0=in0, in1=in1, op=op), r, w)

        def STT(out, in0, sc, in1, op0, op1, r=(), w=()):
            K.op("dve", lambda e: e.scalar_tensor_tensor(out=out, in0=in0, scalar=sc, in1=in1, op0=op0, op1=op1), r, w)

        def ACT(out, in_, func, r=(), w=(), scale=1.0, bias=None):
            if bias is None:
                K.op("act", lambda e: e.activation(out=out, in_=in_, func=func, scale=scale), r, w)
            else:
                K.op("act", lambda e: e.activation(out=out, in_=in_, func=func, scale=scale, bias=bias), r, w)

        def MM(out, lhsT, rhs, start, stop, r=(), w=()):
            K.op("pe", lambda e: e.matmul(out, lhsT=lhsT, rhs=rhs, start=start, stop=stop), r, w)

        def TRP(out, in_, r=(), w=()):
            K.op("pe", lambda e: e.transpose(out=out, in_=in_, identity=IDB), r, w)

        def SCAN(out, d0, d1, r=(), w=()):
            K.op("dve", lambda e: e.tensor_tensor_scan(out=out, data0=d0, data1=d1, initial=1.0, op0=ALU.mult, op1=ALU.add), r, w)

        def CPY(out, in_, r=(), w=(), eng="dve"):
            K.op(eng, lambda e: e.tensor_copy(out=out, in_=in_), r, w)

        def MSET(out, val, w=(), eng="dve"):
            K.op(eng, lambda e: e.memset(out, val), (), w)

        I32 = mybir.dt.int32

        def RSQRT(y, x, t, hx, ty, tx, tt, thx, iters=3):
            TS(y.bitcast(I32), x.bitcast(I32), 1, None, ALU.arith_shift_right, r=[tx], w=[ty])
            TS(y.bitcast(I32), y.bitcast(I32), -1, 0x5f3759df, ALU.mult, ALU.add, r=[ty], w=[ty])
            TS(hx, x, -0.5, None, ALU.mult, r=[tx], w=[thx])
            for _ in range(iters):
                TT(t, y, y, ALU.mult, r=[ty], w=[tt])
                TT(t, t, hx, ALU.mult, r=[tt, thx], w=[tt])
                STT(y, t, 1.5, y, ALU.add, ALU.mult, r=[tt, ty], w=[ty])

        def RCP(out, in_, r=(), w=()):
            K.op("dve", lambda e: e.reciprocal(out=out, in_=in_), r, w)

        wq = []
        wloaded = []
        wctr = [0]

        sctr = [0]
        scr_tok = {}
        spill_q = []

        def flush_spill():
            _, sid_, nel_, b_, bt_ = spill_q.pop(0)
            K.dma(d_scr[sid_][:, :nel_], b_[:, :nel_], bt_, False, w=[scr_tok[sid_]])

        def _load(ap, nel, sid=None, from_scr=False):
            i = wctr[0]
            wctr[0] += 1
            b, bt = WBF[i % 6], tWBF[i % 6]
            if from_scr:
                while spill_q:
                    flush_spill()
                K.dma(b[:, :nel], d_scr[sid][:, :nel], bt, True, r=[scr_tok[sid]])
                return b, bt
            j = sctr[0]
            sctr[0] += 1
            s_, st = STG[j % 2], tSTG[j % 2]
            K.dma(s_[:, :nel], ap, st, True)
            ACT(b[:, :nel], s_[:, :nel], AF.Copy, r=[st], w=[bt])
            if sid is not None:
                scr_tok[sid] = T("scr%d" % sid)
                spill_q.append((i, sid, nel, b, bt))
            while spill_q and i - spill_q[0][0] >= 2:
                flush_spill()
            return b, bt

        def sched(lst):
            wq.extend(lst)

        def next_unit(lookahead=2):
            while len(wloaded) < 1 + lookahead and wq:
                wloaded.append(_load(*wq.pop(0)))
            return wloaded.pop(0)

        def proj(wv, wt, kc, c0, actv, at, col0, n):
            pb, pt = pbank()
            for k in range(kc):
                MM(pb[:, :n], wv[:, k, c0:c0 + 128], actv[:, k, col0:col0 + n], k == 0, k == kc - 1, r=[wt, at], w=[pt])
            return pb, pt

        K.dma(CONS, d_consts, tCONS, True)
        K.dma(SMALL, d_small, tSMALL, True)
        K.dma(CTF, d_cT, tCTF, True)
        K.dma(BADAT, d_badaT, tBADAT, True)
        K.dma(WPGS, d_wpg, tWPGS, True)
        CPY(IDB, CONS[:, 0:128], r=[tCONS], w=[tCB])
        MSET(ONESB, 1.0, w=[tCB])
        MSET(ONES32, 1.0, w=[tCB])
        CPY(SEQMB, CONS[:, 384:400], r=[tCONS], w=[tCB])
        MSET(NHALF, -0.5, w=[tCB])
        MSET(ZERO, 0.0, w=[tCB])
        MSET(S32, 0.0, w=tS32h[0])
        MSET(HIST, 0.0, w=[tHIST])
        ACT(WPG, v3(WPGS, 4), AF.Copy, r=[tWPGS], w=[tWPG])
        ACT(SCT, CTF, AF.Silu, r=[tCTF], w=[tSCT])
        TMP8 = CO[:, 32:40]
        TT(TMP8, SMALL[:, 4:12], SMALL[:, 12:20], ALU.subtract, r=[tSMALL], w=[tCO])
        ACT(TMP8, TMP8, AF.Tanh, r=[tCO], w=[tCO], scale=0.5)
        TS(C0, TMP8, 0.25, 0.75, ALU.mult, ALU.add, r=[tCO], w=[tCO])
        TS(C1, TMP8, -0.25, 0.25, ALU.mult, ALU.add, r=[tCO], w=[tCO])
        TS(NC1, TMP8, 0.25, -0.25, ALU.mult, ALU.add, r=[tCO], w=[tCO])
        TS(NW2, SMALL[:, 20:28], float(np.sqrt(128.0)), None, ALU.mult, r=[tSMALL], w=[tCO])

        def mod_units(u0, u1):
            for u in range(u0, u1):
                wb_, wt_ = next_unit()
                wv = v3(wb_, 8)
                for jj in range(2):
                    pk = 2 * u + jj
                    pb, pt = proj(wv, wt_, 8, jj * 128, SCT, tSCT, 0, 17)
                    TS(MODT[:, pk, :], pb[:, 0:17], BADAT[:, pk:pk + 1], None, ALU.add, r=[pt, tBADAT], w=[tMODT])

        sched([(d_wada[u], 2048, None, False) for u in range(8)])
        mod_units(0, 8)
        TS(MODT[:, 8:16, :], MODT[:, 8:16, :], 1.0, None, ALU.add, r=[tMODT], w=[tMODT])
        EPS2 = LN_EPS / (ALPHA * ALPHA)

        def mcol(pk):
            return MODT[:, pk, 0:1]

        def mrow(pk):
            return MODT[:, pk, 1:17].unsqueeze(2).to_broadcast([128, 16, 8])

        s3 = lambda a: a.rearrange("p (a b) -> p a b", a=16)

        def layer_norm_inplace(c0, n):
            xs = X1[:, :, c0:c0 + n]
            for k in range(8):
                ACT(RB[:, k, :n], X1[:, k, c0:c0 + n], AF.Copy, r=[tX1], w=[tRBk[k]])
                TT(SQ[:, k, :n], X1[:, k, c0:c0 + n], X1[:, k, c0:c0 + n], ALU.mult, r=[tX1], w=[tSQk[k]])
                MM(PS[6][:, :n], ONESB, RB[:, k, :n], k == 0, k == 7, r=[tCB, tRBk[k]], w=[tPS[6]])
                MM(PS[7][:, :n], ONESB, SQ[:, k, :n], k == 0, k == 7, r=[tCB, tSQk[k]], w=[tPS[7]])
            mean, msq, var, rstd = [a[:, :n] for a in ST]
            TS(mean, PS[6][:, :n], 1.0 / 1024, None, ALU.mult, r=[tPS[6]], w=[tST[0]])
            TT(msq, mean, mean, ALU.mult, r=[tST[0]], w=[tST[1]])
            STT(var, PS[7][:, :n], 1.0 / 1024, msq, ALU.mult, ALU.subtract, r=[tPS[7], tST[1]], w=[tST[2]])
            TS(var, var, EPS2, None, ALU.add, r=[tST[2]], w=[tST[2]])
            RSQRT(rstd, var, msq, TM[1][:, :n], tST[3], tST[2], tST[1], tTM[1])
            TT(xs, xs, mean.unsqueeze(1).to_broadcast([128, 8, n]), ALU.subtract, r=[tX1, tST[0]], w=[tX1])
            TT(xs, xs, rstd.unsqueeze(1).to_broadcast([128, 8, n]), ALU.mult, r=[tX1, tST[3]], w=[tX1])

        mctr = [0]
        pending_out = []
        pending_s0 = []

        xctr = [0]

        def load_x_chunk(k, q):
            i = xctr[0] % 3
            xctr[0] += 1
            b0, pl = PASSES[q]
            K.dma(XS[i][:, 0:pl], d_xT[:, k, b0:b0 + pl], tXS[i], True)
            if q == NPASS - 1:
                K.dma(XS[i][:, pl:pl + 128], d_xT[:, k, 2048:2176], tXS[i], True)
            return XS[i], tXS[i]

        def phase1(q):
            fence([tUT] + tHT_all)
            for k in range(8):
                xs_k, txs = load_x_chunk(k, q)
                pl = PASSES[q][1]
                TS(UT[:, k, 0:pl], xs_k[:, 0:pl], mcol(8 + k), mcol(k), ALU.mult, ALU.add, r=[txs, tMODT], w=[tUT])
                if q == NPASS - 1:
                    TT(s3(TM[0][:, 0:128]), s3(xs_k[:, pl:pl + 128]), mrow(8 + k), ALU.mult, r=[txs, tMODT], w=[tTM[0]])
                    TT(s3(UT[:, k, pl:pl + 128]), s3(TM[0][:, 0:128]), mrow(k), ALU.add, r=[tTM[0], tMODT], w=[tUT])

        def sched_pass(q_):
            seq = []
            for h in range(8):
                seq += [(d_win[2 + 2 * h], 2048), (d_win[3 + 2 * h], 2048)]
            ada_at = len(seq)
            seq += [(d_win[0], 2048), (d_win[1], 2048)]
            for u in range(4):
                seq += [(d_wa[u], 1024), (d_win[18 + u], 2048), (d_wb[u], 2048), (d_win[22 + u], 2048)]
            seq += [(d_wo[u], 2048) for u in range(4)]
            for u in range(11):
                seq += [(d_wg[u], 2048), (d_wu[u], 2048)]
            seq += [(d_wd[u], 1408) for u in range(16)]
            assert len(seq) == NSCR
            seq = [(ap, nel, sid, USE_SCR and q_ > 0) for sid, (ap, nel) in enumerate(seq)]
            if not USE_SCR:
                seq = [(ap, nel, None, False) for (ap, nel, _, _) in seq]
            if q_ == 0:
                seq = seq[:ada_at] + [(d_wada[u], 2048, None, False) for u in range(8, 24)] + seq[ada_at:]
            sched(seq)

        for ps_ in range(NPASS):
            last = ps_ == NPASS - 1
            base, plen = PASSES[ps_]
            ptgs = [(c_, min(512, plen - c_), False) for c_ in range(0, plen, 512)]
            tgs = ptgs + ([(plen, 128, True)] if last else [])
            sched_pass(ps_)

            if ps_ == 0:
                phase1(0)
            fence(ln_toks + hg_toks + [tAT, tYHG, tMRG, tYPOOL])
            pnb[0] = 4

            for S_ in HS + [SS]:
                MSET(S_.D1, 0.0, w=[S_.tD1], eng="pool")
            IDN32 = CONS[:, 0:128]
            units = {}

            def head_stream(h, B, c0, n, is_s):
                qf, tqf, vo, tvo = units[h]
                ntile = n // 128
                L, nch = (8, 16) if is_s else (64, n // 64)
                pf, tpf = proj(qf, tqf, 8, 128, UT, tUT, c0, n)
                ACT(B.Fa[:, :n], pf[:, :n], AF.Tanh, r=[tpf], w=[B.tFa], scale=0.5)
                cs_ = lambda a_: a_.rearrange("p (a b) -> p a b", a=nch)[:, :, 0:1]
                D1b, tD1b = (SS.D1, SS.tD1) if is_s else (B.D1, B.tD1)
                ACT(B.rP[:, :n], B.Fa[:, :n], AF.Identity, r=[B.tFa, tCO], w=[B.trP], scale=C1[:, h:h + 1], bias=C0[:, h:h + 1])
                ACT(cs_(D1b[:, :n]), cs_(B.rP[:, :n]), AF.Identity, r=[B.trP], w=[tD1b])
                ACT(cs_(B.rP[:, :n]), cs_(B.rP[:, :n]), AF.Identity, r=[B.trP], w=[B.trP], scale=0.0)
                yield
                SCAN(B.P[:, :n], B.rP[:, :n], D1b[:, :n], r=[B.trP, tD1b], w=[B.tP])
                pq, tpq = proj(qf, tqf, 8, 0, UT, tUT, c0, n)
                ACT(B.Qs[:, :n], pq[:, :n], AF.Silu, r=[tpq], w=[B.tQs]); yield
                ACT(B.Fk[:, :n], B.Fa[:, :n], AF.Identity, r=[B.tFa, tCO], w=[B.tFk], scale=NC1[:, h:h + 1], bias=C1[:, h:h + 1])
                pog, tpog = proj(vo, tvo, 8, 128, UT, tUT, c0, n)
                ACT(B.SG[:, :n], pog[:, :n], AF.Silu, r=[tpog], w=[B.tSG]); yield
                pv, tpv = pbank()
                for ti in range(ntile):
                    for k in range(8):
                        MM(pv[:, ti * 128:(ti + 1) * 128], UT[:, k, c0 + ti * 128:c0 + (ti + 1) * 128], vo[:, k, 0:128], k == 0, k == 7, r=[tUT, tvo], w=[tpv])
                ACT(B.VT[:, :ntile, :], v3(pv[:, :n], ntile), AF.Copy, r=[tpv], w=[B.tVT])
                yield
                RCP(B.rP[:, :n], B.P[:, :n], r=[B.tP], w=[B.trP]); yield
                TT(B.KD[:, :n], B.Fk[:, :n], B.rP[:, :n], ALU.mult, r=[B.tFk, B.trP], w=[B.tKD])
                c3 = lambda a_: a_.rearrange("p (a b) -> p a b", a=nch)
                TT(c3(B.KT[:, :n]), c3(B.KD[:, :n]), c3(B.P[:, :n])[:, :, L - 1:L].to_broadcast([128, nch, L]), ALU.mult, r=[B.tKD, B.tP], w=[B.tKT], eng="pool")
                TT(B.QD[:, :n], B.Qs[:, :n], B.P[:, :n], ALU.mult, r=[B.tQs, B.tP], w=[B.tQD])
                yield
                pat, tpat = pbank() if KVMODE == "batch" else (PS[4], tPS[4])
                for ti in range(ntile):
                    MM(pat[:, ti * 128:(ti + 1) * 128], B.KD[:, ti * 128:(ti + 1) * 128], B.QD[:, ti * 128:(ti + 1) * 128], True, True, r=[B.tKD, B.tQD], w=[tpat])
                MK = MASKS if is_s else MASKP
                TT(B.AM[:, :ntile, :], v3(pat[:, :n], ntile), MK.unsqueeze(1).to_broadcast([128, ntile, 128]), ALU.mult, r=[tpat, tCONS], w=[B.tAM])
                pk_, tpk = pbank()
                pkb = pk_[:].bitcast(BF16).rearrange("p (a b) -> p a b", a=8)
                for ti in range(ntile):
                    TRP(pkb[:, ti, :], B.KT[:, ti * 128:(ti + 1) * 128], r=[B.tKT, tCB], w=[tpk])
                ACT(B.KTT[:, :ntile, :], pkb[:, :ntile, :], AF.Copy, r=[tpk], w=[B.tKTT])
                yield
                if not is_s:
                    ACT(B.SBFC[:, 0, :], S32x[0][:, h, :], AF.Copy, r=[tS32h[0][h]], w=[B.tSBFC])
                    if KVMODE == "batch":
                        for c in range(nch):
                            ti, hf = c // 2, c % 2
                            bi = B.kvb[c // 4]
                            sl = PS[bi][:, (c % 4) * 128:(c % 4 + 1) * 128]
                            MM(sl, B.KTT[hf * 64:(hf + 1) * 64, ti, :], B.VT[hf * 64:(hf + 1) * 64, ti, :], True, True, r=[B.tKTT, B.tVT], w=[tPS[bi]])
                        yield
                    def kv_mm(c):
                        ti, hf = c // 2, c % 2
                        bi = 6 + c % 2
                        sl = PS[bi][:, (2 * B.sidx + (c // 2) % 2) * 128:(2 * B.sidx + (c // 2) % 2 + 1) * 128]
                        MM(sl, B.KTT[hf * 64:(hf + 1) * 64, ti, :], B.VT[hf * 64:(hf + 1) * 64, ti, :], True, True, r=[B.tKTT, B.tVT], w=[tPS[bi]])

                    if KVMODE != "batch":
                        kv_mm(0)
                        kv_mm(1)
                    for c in range(nch):
                        if KVMODE == "batch":
                            bi = B.kvb[c // 4]
                            sl = PS[bi][:, (c % 4) * 128:(c % 4 + 1) * 128]
                        else:
                            bi = 6 + c % 2
                            sl = PS[bi][:, (2 * B.sidx + (c // 2) % 2) * 128:(2 * B.sidx + (c // 2) % 2 + 1) * 128]
                        si, so = c % 2, (c + 1) % 2
                        STT(S32x[so][:, h, :], S32x[si][:, h, :], B.P[:, c * 64 + 63:c * 64 + 64], sl, ALU.mult, ALU.add,
                            r=[tS32h[si][h], B.tP, tPS[bi]], w=[tS32h[so][h]])
                        ACT(B.SBFC[:, c + 1, :], S32x[so][:, h, :], AF.Copy, r=[tS32h[so][h]], w=[B.tSBFC])
                        if KVMODE != "batch" and c + 2 < nch:
                            kv_mm(c + 2)
                        yield
                    po, tpo = pbank() if KVMODE == "batch" else (PS[5], tPS[5])
                    for ti in range(ntile):
                        MM(po[:, ti * 128:(ti + 1) * 128], B.VT[:, ti, :], B.AM[:, ti, :], True, False, r=[B.tVT, B.tAM], w=[tpo])
                        for hf in range(2):
                            c = 2 * ti + hf
                            MM(po[:, c * 64:(c + 1) * 64], B.SBFC[:, c, :], B.QD[:, c * 64:(c + 1) * 64], False, hf == 1, r=[B.tSBFC, B.tQD], w=[tpo])
                    ACT(B.OSQ[:, :n], po[:, :n], AF.Square, r=[tpo], w=[B.tOSQ])
                    ACT(B.O32[:, :n], po[:, :n], AF.Identity, r=[tpo, tCO], w=[B.tO32], scale=NW2[:, h:h + 1])
                    yield
                else:
                    while pending_s0:
                        K.dma(pending_s0.pop(0), S0, tS0, False)
                    K.dma(S0, d_shg[:, h].rearrange("n k v -> k n v"), tS0, True)
                    ACT(S0B, S0, AF.Copy, r=[tS0], w=[tS0B]); yield
                    po, tpo = pbank() if KVMODE == "batch" else (PS[5], tPS[5])
                    MM(po[:, 0:128], B.VT[:, 0, :], B.AM[:, 0, :], True, False, r=[B.tVT, B.tAM], w=[tpo])
                    for sq_ in range(16):
                        MM(po[:, sq_ * 8:(sq_ + 1) * 8], S0B[:, sq_, :], B.QD[:, sq_ * 8:(sq_ + 1) * 8], False, sq_ == 15, r=[tS0B, B.tQD], w=[tpo])
                    ACT(B.O32[:, :n], po[:, :n], AF.Identity, r=[tpo, tCO], w=[B.tO32], scale=NW2[:, h:h + 1])
                    ACT(B.OSQ[:, :n], po[:, :n], AF.Square, r=[tpo], w=[B.tOSQ])
                    yield
                    TT(VEXP, B.VT[:, 0, :].unsqueeze(1).to_broadcast([128, 16, 128]), SEQMB.unsqueeze(2).to_broadcast([128, 16, 128]), ALU.mult, r=[B.tVT, tCB], w=[tVEXP])
                    Ps3 = B.P[:, :128].rearrange("p (a b) -> p a b", a=16)
                    TT(S0, S0, Ps3[:, :, 7:8].to_broadcast([128, 16, 128]), ALU.mult, r=[tS0, B.tP], w=[tS0], eng="pool")
                    yield
                    for qd_ in range(4):
                        pbk, tbk = pbank()
                        MM(pbk[:, :], B.KTT[:, 0, :], VEXP[:, qd_ * 4:(qd_ + 1) * 4, :], True, True, r=[B.tKTT, tVEXP], w=[tbk])
                        TT(S0[:, qd_ * 4:(qd_ + 1) * 4, :], S0[:, qd_ * 4:(qd_ + 1) * 4, :], v3(pbk[:, :], 4), ALU.add, r=[tS0, tbk], w=[tS0])
                        yield
                    pending_s0.append(o_nhs[:, h].rearrange("n k v -> k n v"))
                pss, tpss = pbank()
                for ti in range(ntile):
                    MM(pss[:, ti:ti + 1], B.OSQ[:, ti * 128:(ti + 1) * 128], ONESB[:, 0:1], True, True, r=[B.tOSQ, tCB], w=[tpss])
                xs_, ys_, ts_, hs_ = [B.RSs[:, i * 4:i * 4 + ntile] for i in range(4)]
                TS(xs_, pss[:, 0:ntile], 128.0 * RMS_EPS, None, ALU.add, r=[tpss], w=[B.tRSs]); yield
                TS(ys_.bitcast(I32), xs_.bitcast(I32), 1, None, ALU.arith_shift_right, r=[B.tRSs], w=[B.tRSs])
                TS(ys_.bitcast(I32), ys_.bitcast(I32), -1, 0x5f3759df, ALU.mult, ALU.add, r=[B.tRSs], w=[B.tRSs])
                TS(hs_, xs_, -0.5, None, ALU.mult, r=[B.tRSs], w=[B.tRSs]); yield
                for _ in range(3):
                    TT(ts_, ys_, ys_, ALU.mult, r=[B.tRSs], w=[B.tRSs])
                    TT(ts_, ts_, hs_, ALU.mult, r=[B.tRSs], w=[B.tRSs])
                    STT(ys_, ts_, 1.5, ys_, ALU.add, ALU.mult, r=[B.tRSs], w=[B.tRSs]); yield
                bmv = B.BM[:, :, :].rearrange("p a b -> p (a b)").bitcast(BF16)
                nn = ntile * 128
                bmh = bmv[:, 0:nn].rearrange("p (a b) -> p a b", a=ntile)
                bml = bmv[:, nn:2 * nn].rearrange("p (a b) -> p a b", a=ntile)
                hi_b = B.RSs[:, 8:8 + 2].bitcast(BF16)[:, 0:ntile]
                CPY(hi_b, ys_, r=[B.tRSs], w=[B.tRSs])
                TT(hs_, ys_, hi_b, ALU.subtract, r=[B.tRSs], w=[B.tRSs])
                TT(bmh, IDN32.unsqueeze(1).to_broadcast([128, ntile, 128]), hi_b.unsqueeze(2).to_broadcast([128, ntile, 128]), ALU.mult,
                   r=[tCONS, B.tRSs], w=[B.tBM], eng="pool")
                TT(bml, IDN32.unsqueeze(1).to_broadcast([128, ntile, 128]), hs_.unsqueeze(2).to_broadcast([128, ntile, 128]), ALU.mult,
                   r=[tCONS, B.tRSs], w=[B.tBM2])
                yield
                pbc, tpbc = pbank()
                MM(pbc[:, :n], ONESB, bmv[:, 0:nn], True, False, r=[tCB, B.tBM], w=[tpbc])
                MM(pbc[:, :n], ONESB, bmv[:, nn:2 * nn], False, True, r=[tCB, B.tBM2], w=[tpbc])
                TT(B.O32[:, :n], B.O32[:, :n], pbc[:, :n], ALU.mult, r=[B.tO32, tpbc], w=[B.tO32]); yield
                TT(YHG[:, h, c0:c0 + n], B.O32[:, :n], B.SG[:, :n], ALU.mult, r=[B.tO32, B.tSG], w=[tYHG])
                yield

            def get_units(h):
                qf_, tqf = next_unit()
                vo_, tvo = next_unit()
                units[h] = (v3(qf_, 8), tqf, v3(vo_, 8), tvo)

            def chain(*gs):
                for g_ in gs:
                    yield from g_

            def zip_run(gens):
                gens = list(gens)
                if not ZIP:
                    for g_ in gens:
                        for _ in g_:
                            pass
                    return
                while gens:
                    for g_ in list(gens):
                        try:
                            next(g_)
                        except StopIteration:
                            gens.remove(g_)

            sbusy = [False]

            def stream_chain(heads, B):
                for h_ in heads:
                    get_units(h_)
                    for (c0_, n_, _) in ptgs:
                        yield from head_stream(h_, B, c0_, n_, False)
                    if last:
                        while sbusy[0]:
                            yield
                        sbusy[0] = True
                        yield from head_stream(h_, B, plen, 128, True)
                        sbusy[0] = False

            gA = stream_chain([0, 2, 4, 6], HS[0])
            gB = stream_chain([1, 3, 5, 7], HS[1])
            for _ in range(STAGGER):
                next(gA)
            while pending_out:
                dst_, src_ = pending_out.pop(0)
                K.dma(dst_, src_, tX1, False)
            zip_run([gA, gB])
            while pending_s0:
                K.dma(pending_s0.pop(0), S0, tS0, False)
            if last:
                for h in range(8):
                    K.dma(o_nhp[h], S32[:, h, :], tS32h[0][h], False)
            fence(hg_toks + pl_toks)
            pnb[0] = 6
            if ps_ == 0:
                mod_units(8, 24)
                TS(MODT[:, 32:40, :], MODT[:, 32:40, :], 1.0, None, ALU.add, r=[tMODT], w=[tMODT])
                TS(MODT[:, 16:24, :], MODT[:, 16:24, :], 0.5 / ALPHA, None, ALU.mult, r=[tMODT], w=[tMODT])
                TS(MODT[:, 40:48, :], MODT[:, 40:48, :], 1.0 / ALPHA, None, ALU.mult, r=[tMODT], w=[tMODT])

            for g in range(4):
                if g % 2 == 0:
                    pw_, tpw = next_unit(); pw = v3(pw_, 8)
                w = 2 << g
                gc = (g % 2) * 128
                for (c0_, n_, _) in ptgs:
                    pz, tpz = proj(pw, tpw, 8, gc, UT, tUT, c0_, n_)
                    CPY(ZP[:, 0:16], HIST[:, g, :], r=[tHIST], w=[tZP])
                    ACT(ZP[:, 16:16 + n_], pz[:, :n_], AF.Copy, r=[tpz], w=[tZP])
                    CPY(HIST[:, g, :], ZP[:, n_:n_ + 16], r=[tZP], w=[tHIST])
                    cur, tcur = ZP, tZP
                    for j in range(g + 1):
                        sh = 1 << j
                        nx, tnx = (PA, tPA) if j % 2 == 0 else (PB, tPB)
                        TT(nx[:, sh:16 + n_], cur[:, sh:16 + n_], cur[:, 0:16 + n_ - sh], ALU.add, r=[tcur], w=[tnx])
                        cur, tcur = nx, tnx
                    STT(MT[:, c0_:c0_ + n_], cur[:, 16:16 + n_], 1.0 / w, ZP[:, 16:16 + n_], ALU.mult, ALU.subtract, r=[tcur, tZP], w=[tMT])
                    if ps_ == 0 and c0_ == 0:
                        TT(T16, cur[:, 16:32], INVC[:, g, :], ALU.mult, r=[tcur, tCONS], w=[tT16])
                        TT(MT[:, 0:16], T16, ZP[:, 16:32], ALU.subtract, r=[tT16, tZP, tMT], w=[tMT])
                    if last and c0_ + n_ == plen:
                        K.dma(o_zpp[:, g, :], ZP[:, n_ + 1:n_ + 16], tZP, False)
                if last:
                    MSET(ZPS, 0.0, w=[tZPS])
                    K.dma(ZPS[:, :, 1:16], d_spT[:, g], tZPS, True)
                    pzs, tpzs = proj(pw, tpw, 8, gc, UT, tUT, plen, 128)
                    ACT(ZPS[:, :, 16:24], s3(pzs[:, 0:128]), AF.Copy, r=[tpzs], w=[tZPS])
                    cur, tcur = ZPS, tZPS
                    for j in range(g + 1):
                        sh = 1 << j
                        nx, tnx = (PAS, tPAS) if j % 2 == 0 else (PBS, tPBS)
                        TT(nx[:, :, sh:24], cur[:, :, sh:24], cur[:, :, 0:24 - sh], ALU.add, r=[tcur], w=[tnx])
                        cur, tcur = nx, tnx
                    STT(s3(MT[:, plen:plen + 128]), cur[:, :, 16:24], 1.0 / w, ZPS[:, :, 16:24], ALU.mult, ALU.subtract, r=[tcur, tZPS, tMT], w=[tMT])
                    K.dma(o_zps[:, g], ZPS[:, :, 9:24], tZPS, False)
                for (c0, n, is_s) in tgs:
                    pb, pt = pbank()
                    MM(pb[:, :n], WPG[:, g, :], MT[:, c0:c0 + n], True, True, r=[tWPG, tMT], w=[pt])
                    ACT(YPOOL[:, g, c0:c0 + n], pb[:, :n], AF.Identity, r=[pt, tSMALL], w=[tYPOOL], scale=PSC[:, g:g + 1])
            fence(pl_toks + ln_toks)

            for u in range(4):
                wa_, twa = next_unit(); wa = v3(wa_[:, :1024], 4)
                ga_, tga = next_unit(); ga = v3(ga_, 8)
                wb_, twb = next_unit(); wbv = v3(wb_, 8)
                gb_, tgb = next_unit(); gb = v3(gb_, 8)
                for jj in range(2):
                    j = 2 * u + jj
                    for (c0, n, is_s) in tgs:
                        pa, tpa = proj(wa, twa, 4, jj * 128, YPOOL, tYPOOL, c0, n)
                        pg, tpg = proj(ga, tga, 8, jj * 128, UT, tUT, c0, n)
                        mctr[0] += 1
                        ia, ib = (mctr[0] % 2) * 2, (mctr[0] % 2) * 2 + 1
                        ACT(TA[ia][:, :n], pg[:, :n], AF.Tanh, r=[tpg], w=[tTA[ia]], scale=0.5)
                        STT(TM[0][:, :n], TA[ia][:, :n], 1.0, pa[:, :n], ALU.add, ALU.mult, r=[tTA[ia], tpa], w=[tTM[0]])
                        pb2, tpb2 = proj(wbv, twb, 8, jj * 128, YHG, tYHG, c0, n)
                        pg2, tpg2 = proj(gb, tgb, 8, jj * 128, UT, tUT, c0, n)
                        ACT(TA[ib][:, :n], pg2[:, :n], AF.Tanh, r=[tpg2], w=[tTA[ib]], scale=0.5)
                        STT(TM[1][:, :n], TA[ib][:, :n], 1.0, pb2[:, :n], ALU.add, ALU.mult, r=[tTA[ib], tpb2], w=[tTM[1]])
                        TT(MRG[:, j, c0:c0 + n], TM[0][:, :n], TM[1][:, :n], ALU.add, r=[tTM[0], tTM[1]], w=[tMRG], eng="pool")

            for u in range(4):
                wo_, two = next_unit(); wo = v3(wo_, 8)
                for jj in range(2):
                    j = 2 * u + jj
                    xs_j, txs_j = load_x_chunk(j, ps_)
                    for (c0, n, is_s) in tgs:
                        pm, tpm = proj(wo, two, 8, jj * 128, MRG, tMRG, c0, n)
                        if not is_s:
                            STT(X1[:, j, c0:c0 + n], pm[:, :n], mcol(16 + j), xs_j[:, c0:c0 + n], ALU.mult, ALU.add, r=[tpm, tMODT, txs_j], w=[tX1])
                        else:
                            TT(s3(TM[0][:, 0:128]), s3(pm[:, 0:128]), mrow(16 + j), ALU.mult, r=[tpm, tMODT], w=[tTM[0]])
                            TT(X1[:, j, c0:c0 + 128], TM[0][:, 0:128], xs_j[:, c0:c0 + 128], ALU.add, r=[tTM[0], txs_j], w=[tX1])
            fence([tUT] + tHT_all)
            for (c0, n, is_s) in tgs:
                layer_norm_inplace(c0, n)
                for k in range(8):
                    TS(X1[:, k, c0:c0 + n], X1[:, k, c0:c0 + n], LG1[:, k:k + 1], LB1[:, k:k + 1], ALU.mult, ALU.add, r=[tX1, tSMALL], w=[tX1])
                    if not is_s:
                        ACT(HT[:, k, c0:c0 + n], X1[:, k, c0:c0 + n], AF.Identity, r=[tX1, tMODT], w=[tHTg[c0]], scale=mcol(32 + k), bias=mcol(24 + k))
                    else:
                        TT(s3(TM[0][:, 0:128]), s3(X1[:, k, c0:c0 + 128]), mrow(32 + k), ALU.mult, r=[tX1, tMODT], w=[tTM[0]])
                        TT(s3(HT[:, k, c0:c0 + 128]), s3(TM[0][:, 0:128]), mrow(24 + k), ALU.add, r=[tTM[0], tMODT], w=[tHTg[c0]])
            fence([tYHG, tMRG, tYPOOL, tAT])

            for u in range(11):
                wg_, twg = next_unit(); wg = v3(wg_, 8)
                wu_, twu = next_unit(); wu = v3(wu_, 8)
                for jj in range(2):
                    fc = 2 * u + jj
                    for (c0, n, is_s) in tgs:
                        pg, tpg = proj(wg, twg, 8, jj * 128, HT, tHTg[c0], c0, n)
                        pu, tpu = proj(wu, twu, 8, jj * 128, HT, tHTg[c0], c0, n)
                        mctr[0] += 1
                        i2 = mctr[0] % 4
                        ACT(TA[i2][:, :n], pg[:, :n], AF.Silu, r=[tpg], w=[tTA[i2]])
                        TT(AT[:, fc, c0:c0 + n], TA[i2][:, :n], pu[:, :n], ALU.mult, r=[tTA[i2], tpu], w=[tAT])
            if not last:
                phase1(ps_ + 1)
            for j in range(8):
                wda_, twda = next_unit(); wda = v3(wda_[:, :1408], 11)
                wdb_, twdb = next_unit(); wdb = v3(wdb_[:, :1408], 11)
                for (c0, n, is_s) in tgs:
                    pd, tpd = pbank()
                    for kc in range(22):
                        wv, wt = (wda, twda) if kc < 11 else (wdb, twdb)
                        MM(pd[:, :n], wv[:, kc % 11, :], AT[:, kc, c0:c0 + n], kc == 0, kc == 21, r=[wt, tAT], w=[tpd])
                    if not is_s:
                        STT(X1[:, j, c0:c0 + n], pd[:, :n], mcol(40 + j), X1[:, j, c0:c0 + n], ALU.mult, ALU.add, r=[tpd, tMODT, tX1], w=[tX1])
                    else:
                        TT(s3(TM[0][:, 0:128]), s3(pd[:, 0:128]), mrow(40 + j), ALU.mult, r=[tpd, tMODT], w=[tTM[0]])
                        TT(X1[:, j, c0:c0 + 128], TM[0][:, 0:128], X1[:, j, c0:c0 + 128], ALU.add, r=[tTM[0], tX1], w=[tX1])
            for (c0, n, is_s) in tgs:
                layer_norm_inplace(c0, n)
                for k in range(8):
                    TS(X1[:, k, c0:c0 + n], X1[:, k, c0:c0 + n], LG2[:, k:k + 1], LB2[:, k:k + 1], ALU.mult, ALU.add, r=[tX1, tSMALL], w=[tX1])
                dcol = 2048 if is_s else base + c0
                if last:
                    K.dma(o_yT[:, :, dcol:dcol + n], X1[:, :, c0:c0 + n], tX1, False)
                else:
                    pending_out.append((o_yT[:, :, dcol:dcol + n], X1[:, :, c0:c0 + n]))

        final = [tX1, tZP, tZPS, tS0] + tS32h[0]
        K.emit(es, final_toks=[t for t in final if t.dsem is not None])
    return nc


def _units(w, kc, ncol):
    Kd, N = w.shape
    assert Kd == kc * 128
    nu = N // ncol
    return np.ascontiguousarray(w.reshape(kc, 128, nu, ncol).transpose(2, 1, 0, 3).reshape(nu, 128, kc * ncol))


_NC_CACHE = {}


def prep_inputs(x_prompt, x_sample, state_pool, state_hgrn, c_prompt, c_sample,
                w_ada, b_ada, w_in, w_pool_grp, pool_scale, lb_logits, hgrn_norm_w,
                w_a, w_b, w_out, ln1_g, ln1_b, w_gate, w_up, w_down, ln2_g, ln2_b, cores=None):
    f = lambda a: np.asarray(a, dtype=np.float32)
    x_prompt, x_sample, state_pool, state_hgrn = f(x_prompt), f(x_sample), f(state_pool), f(state_hgrn)
    c_prompt, c_sample = f(c_prompt), f(c_sample)
    w_ada, b_ada, w_in = f(w_ada)[0], f(b_ada)[0], f(w_in)[0]
    w_pool_grp, pool_scale, lb_logits, hgrn_norm_w = f(w_pool_grp)[0], f(pool_scale)[0], f(lb_logits), f(hgrn_norm_w)[0]
    w_a, w_b, w_out = f(w_a)[0], f(w_b)[0], f(w_out)[0]
    w_gate, w_up, w_down = f(w_gate)[0], f(w_up)[0], f(w_down)[0]
    ln1_g, ln1_b, ln2_g, ln2_b = f(ln1_g)[0], f(ln1_b)[0], f(ln2_g)[0], f(ln2_b)[0]

    colT = lambda v, nk: np.ascontiguousarray(v.reshape(nk, 128).T)
    wada_u = _units(w_ada, 8, 256)
    Pq, Pf, Pi, Pog, Pga, Pgb = 512, 1536, 2560, 3584, 4608, 5632
    cols = list(range(0, 512))
    for h in range(8):
        cols += list(range(Pq + h * 128, Pq + (h + 1) * 128)) + list(range(Pf + h * 128, Pf + (h + 1) * 128))
        cols += list(range(Pi + h * 128, Pi + (h + 1) * 128)) + list(range(Pog + h * 128, Pog + (h + 1) * 128))
    cols += list(range(Pga, Pga + 1024)) + list(range(Pgb, Pgb + 1024))
    win_u = _units(w_in[:, np.array(cols)], 8, 256)
    wpg = np.ascontiguousarray(w_pool_grp.transpose(1, 0, 2).reshape(128, 512))
    wa_u = _units(w_a, 4, 256)
    wb_u = _units(w_b, 8, 256)
    wo_u = _units(w_out, 8, 256)
    wg_u = _units(w_gate, 8, 256)
    wu_u = _units(w_up, 8, 256)
    wd4 = w_down.reshape(2, 11, 128, 8, 128)
    wd_u = np.ascontiguousarray(wd4.transpose(3, 0, 2, 1, 4).reshape(16, 128, 1408))
    small = np.zeros((128, 64), np.float32)
    small[:, 0:4] = colT(pool_scale, 4)
    small[:, 4:12] = colT(lb_logits[0], 8)
    small[:, 12:20] = colT(lb_logits[1], 8)
    small[:, 20:28] = colT(hgrn_norm_w, 8)
    small[:, 28:36] = colT(ln1_g, 8)
    small[:, 36:44] = colT(ln1_b, 8)
    small[:, 44:52] = colT(ln2_g, 8)
    small[:, 52:60] = colT(ln2_b, 8)
    badaT = colT(b_ada, 48)
    consts = np.zeros((128, 464), np.float32)
    consts[:, 0:128] = np.eye(128, dtype=np.float32)
    s_i = np.arange(128)[:, None]
    t_i = np.arange(128)[None, :]
    consts[:, 128:256] = ((s_i // 64 == t_i // 64) & (s_i <= t_i)).astype(np.float32)
    consts[:, 256:384] = ((s_i // 8 == t_i // 8) & (s_i <= t_i)).astype(np.float32)
    consts[:, 384:400] = (s_i // 8 == np.arange(16)[None, :]).astype(np.float32)
    for g in range(4):
        wdw = 2 << g
        consts[:, 400 + g * 16:416 + g * 16] = (1.0 / np.minimum(np.arange(16) + 1, wdw)).astype(np.float32)[None, :]

    in_maps = []
    for c in (range(NCORE) if cores is None else cores):
        xs = x_sample[c * 16:(c + 1) * 16].reshape(128, 1024)
        xall = np.concatenate([x_prompt[c], xs], axis=0)
        xT = np.ascontiguousarray(xall.T.reshape(8, 128, 2176).transpose(1, 0, 2))
        call = np.concatenate([c_prompt[c:c + 1], c_sample[c * 16:(c + 1) * 16]], axis=0)
        cT = np.ascontiguousarray(call.T.reshape(8, 128, 17).transpose(1, 0, 2))
        sp = state_pool[0, c * 16:(c + 1) * 16]
        spT = np.ascontiguousarray(sp.reshape(16, 15, 4, 128).transpose(3, 2, 0, 1))
        shg = np.ascontiguousarray(state_hgrn[0, c * 16:(c + 1) * 16])
        in_maps.append({"xT": xT, "cT": cT, "badaT": badaT, "wada": wada_u, "win": win_u, "wpg": wpg, "wa": wa_u, "wb": wb_u,
                        "wo": wo_u, "wg": wg_u, "wu": wu_u, "wd": wd_u, "small": small, "consts": consts, "spT": spT, "shg": shg})
    return in_maps


def assemble_core(r):
    yT = np.asarray(r["yT"], dtype=np.float32)
    yall = yT.transpose(2, 1, 0).reshape(2176, 1024)
    y_p = yall[:2048]
    y_s = yall[2048:].reshape(16, 8, 1024)
    npp = np.asarray(r["zpp"], dtype=np.float32).transpose(2, 1, 0).reshape(15, 512)
    nps = np.asarray(r["zps"], dtype=np.float32).transpose(2, 3, 1, 0).reshape(16, 15, 512)
    return y_p, y_s, npp, np.asarray(r["nhp"], dtype=np.float32), nps, np.asarray(r["nhs"], dtype=np.float32)


def kernel(**inputs):
    in_maps = prep_inputs(**inputs)
    if "nc" not in _NC_CACHE:
        _NC_CACHE["nc"] = build_nc()
    res = run_bass_kernel_spmd(_NC_CACHE["nc"], in_maps, core_ids=list(range(NCORE)))
    R = res.results
    y_p = np.zeros((8, 2048, 1024), np.float32)
    y_s = np.zeros((128, 8, 1024), np.float32)
    npp = np.zeros((1, 8, 15, 512), np.float32)
    nhp = np.zeros((1, 8, 8, 128, 128), np.float32)
    nps = np.zeros((1, 128, 15, 512), np.float32)
    nhs = np.zeros((1, 128, 8, 128, 128), np.float32)
    for c in range(NCORE):
        a, b, d, e, g, h = assemble_core(R[c])
        y_p[c] = a
        y_s[c * 16:(c + 1) * 16] = b
        npp[0, c] = d
        nhp[0, c] = e
        nps[0, c * 16:(c + 1) * 16] = g
        nhs[0, c * 16:(c + 1) * 16] = h
    return (y_p, y_s, npp, nhp, nps, nhs)
```

```python
import numpy as np
from contextlib import ExitStack
import concourse.bass as bass
import concourse.mybir as mybir
from concourse.bass_utils import run_bass_kernel_spmd

F32 = mybir.dt.float32
BF16 = mybir.dt.bfloat16
AF = mybir.ActivationFunctionType
ALU = mybir.AluOpType

ALPHA = 2.0 ** 0.25
LN_EPS = 1e-5
RMS_EPS = 1e-6
NCORE = 8
TL = 768
NPASS = 3
PASSES = [(0, 768), (768, 768), (1536, 512)]
RSTD_MODE = "tokb"
ZIP = True
KVMODE = "alt"
USE_SCR = True
STAGGER = 32


class Tok:
    __slots__ = ("name", "lw", "rd", "dsem", "dcount")

    def __init__(self, name):
        self.name = name
        self.lw = None
        self.rd = {}
        self.dsem = None
        self.dcount = 0


class Op:
    __slots__ = ("eng", "fn", "deps", "inc", "val", "dtok", "dval")

    def __init__(self, eng, fn):
        self.eng = eng
        self.fn = fn
        self.deps = []
        self.inc = False
        self.val = 0
        self.dtok = None
        self.dval = 0


class Trk:
    ENGS = ("pe", "act", "dve", "pool", "sp")

    def __init__(self, nc):
        self.nc = nc
        self.ops = {e: [] for e in self.ENGS}
        self.dtoks = []

    def tok(self, name="t"):
        return Tok(name)

    def _collect(self, eng, r, w):
        deps = {}

        def add(ref, raw):
            if ref is None:
                return
            if ref[0] == "e":
                o2 = ref[1]
                if o2.eng == eng and eng == "pe":
                    return
                deps[id(o2)] = ref
            else:
                key = ("d", id(ref[1]))
                if key not in deps or deps[key][2] < ref[2]:
                    deps[key] = ref

        for t in r:
            add(t.lw, True)
        for t in w:
            add(t.lw, False)
            for x in t.rd.values():
                add(x, False)
        return list(deps.values())

    def op(self, eng, fn, r=(), w=()):
        o = Op(eng, fn)
        o.deps = self._collect(eng, r, w)
        for d in o.deps:
            if d[0] == "e":
                d[1].inc = True
        ref = ("e", o)
        for t in w:
            t.lw = ref
            t.rd = {}
        for t in r:
            t.rd[eng] = ref
        self.ops[eng].append(o)
        return o

    def dma(self, out, in_, tok, write, r=(), w=()):
        q = "sp"
        rr = list(r) + ([] if write else [tok])
        ww = list(w) + ([tok] if write else [])
        o = Op(q, lambda e: e.dma_start(out=out, in_=in_))
        o.deps = self._collect(q, rr, ww)
        for d in o.deps:
            if d[0] == "e":
                d[1].inc = True
        if tok.dsem is None:
            tok.dsem = "pending"
            self.dtoks.append(tok)
        tok.dcount += 16
        o.dtok = tok
        o.dval = tok.dcount
        ref = ("d", tok, tok.dcount)
        for t in ww:
            t.lw = ref
            t.rd = {}
        for t in rr:
            t.rd[("d", id(tok))] = ref
        self.ops[q].append(o)
        return o

    def emit(self, es, final_toks=()):
        nc = self.nc
        esem = {e: es.enter_context(nc.semaphore("s_" + e)) for e in self.ENGS}
        for i, t in enumerate(self.dtoks):
            t.dsem = es.enter_context(nc.semaphore("d%d" % i))
        for e in self.ENGS:
            c = 0
            for o in self.ops[e]:
                if o.inc:
                    c += 1
                    o.val = c
        fin = Op("sp", None)
        fin.deps = [("d", t, t.dcount) for t in final_toks]
        self.ops["sp"].append(fin)
        block = es.enter_context(nc.Block())
        ops = self.ops

        def run(e, name):
            seen = {}
            for o in ops[name]:
                for d in o.deps:
                    if d[0] == "e":
                        sem, val, key = esem[d[1].eng], d[1].val, d[1].eng
                    else:
                        sem, val, key = d[1].dsem, d[2], id(d[1])
                    if seen.get(key, 0) >= val:
                        continue
                    e.wait_ge(sem, val)
                    seen[key] = val
                if o.fn is None:
                    continue
                ins = o.fn(e)
                if o.dtok is not None:
                    ins.then_inc(o.dtok.dsem, 16)
                elif o.inc:
                    ins.then_inc(esem[name], 1)

        @block.tensor
        def _(e):
            run(e, "pe")

        @block.scalar
        def _(e):
            run(e, "act")

        @block.vector
        def _(e):
            run(e, "dve")

        @block.gpsimd
        def _(e):
            run(e, "pool")

        @block.sync
        def _(e):
            run(e, "sp")


def build_nc():
    nc = bass.Bass("TRN2", target_bir_lowering=False)
    din = lambda n, s: nc.dram_tensor(n, list(s), F32, kind="ExternalInput").ap()
    dout = lambda n, s: nc.dram_tensor(n, list(s), F32, kind="ExternalOutput").ap()
    d_xT = din("xT", [128, 8, 2176])
    d_cT = din("cT", [128, 8, 17])
    d_badaT = din("badaT", [128, 48])
    d_wada = din("wada", [24, 128, 2048])
    d_win = din("win", [26, 128, 2048])
    d_wpg = din("wpg", [128, 512])
    d_wa = din("wa", [4, 128, 1024])
    d_wb = din("wb", [4, 128, 2048])
    d_wo = din("wo", [4, 128, 2048])
    d_wg = din("wg", [11, 128, 2048])
    d_wu = din("wu", [11, 128, 2048])
    d_wd = din("wd", [16, 128, 1408])
    d_small = din("small", [128, 64])
    d_consts = din("consts", [128, 464])
    d_spT = din("spT", [128, 4, 16, 15])
    d_shg = din("shg", [16, 8, 128, 128])
    o_yT = dout("yT", [128, 8, 2176])
    o_zpp = dout("zpp", [128, 4, 15])
    o_zps = dout("zps", [128, 4, 16, 15])
    o_nhp = dout("nhp", [8, 128, 128])
    o_nhs = dout("nhs", [16, 8, 128, 128])

    NSCR = 76
    d_scr = nc.dram_tensor("wscr", [NSCR, 128, 2048], BF16, kind="Internal").ap()

    es = ExitStack()
    with es:
        K = Trk(nc)
        T = K.tok
        sb = lambda n, s, d=F32: es.enter_context(nc.sbuf_tensor(n, list(s), d))
        ARENA_F = 53200
        arena = sb("arena", [128, ARENA_F], F32)
        apos = [0]

        def carve_at(off, nel, dt):
            nb = nel * (2 if dt == BF16 else 4)
            assert off % 4 == 0
            a = arena[:, off // 4:(off + nb + 3) // 4]
            if dt == BF16:
                a = a.bitcast(BF16)[:, :nel]
            return a

        def carve(nel, dt):
            off = apos[0]
            nb = (nel * (2 if dt == BF16 else 4) + 63) // 64 * 64
            apos[0] += nb
            assert apos[0] <= ARENA_F * 4, "arena overflow"
            return carve_at(off, nel, dt)

        v3 = lambda a, n0: a.rearrange("p (a b) -> p a b", a=n0)
        UT = v3(carve(8 * TL, BF16), 8); tUT = T("UT")
        HT = UT
        tHTg = {c_: T("HT%d" % c_) for c_ in (0, 512)}
        tHT_all = list(tHTg.values())
        X1 = v3(carve(8 * TL, F32), 8); tX1 = T("X1")
        XS = [carve(TL, F32) for _ in range(3)]; tXS = [T("XS%d" % i) for i in range(3)]
        at_off = apos[0]
        AT = v3(carve(22 * TL, BF16), 22); tAT = T("AT")
        apos[0] = at_off
        YHG = v3(carve(8 * TL, BF16), 8); tYHG = T("YHG")
        MRG = v3(carve(8 * TL, BF16), 8); tMRG = T("MRG")
        YPOOL = v3(carve(4 * TL, BF16), 4); tYPOOL = T("YPOOL")
        apos[0] = max(apos[0], at_off + 22 * TL * 2)
        rc_off = apos[0]
        class HSet:
            pass

        def mk_set(wd, nt, nm):
            S_ = HSet()
            S_.Fa = carve(wd, F32); S_.tFa = T(nm + "Fa")
            S_.Fk = carve(wd, F32); S_.tFk = T(nm + "Fk")
            S_.P = carve(wd, F32); S_.tP = T(nm + "P")
            S_.rP = carve(wd, F32); S_.trP = T(nm + "rP")
            S_.Qs = carve(wd, F32); S_.tQs = T(nm + "Qs")
            S_.KD = carve(wd, BF16); S_.tKD = T(nm + "KD")
            S_.KT = carve(wd, BF16); S_.tKT = T(nm + "KT")
            S_.QD = carve(wd, BF16); S_.tQD = T(nm + "QD")
            S_.VT = v3(carve(wd, BF16), nt); S_.tVT = T(nm + "VT")
            S_.SG = carve(wd, F32); S_.tSG = T(nm + "SG")
            S_.AM = v3(carve(wd, BF16), nt); S_.tAM = T(nm + "AM")
            S_.KTT = v3(carve(wd, BF16), nt); S_.tKTT = T(nm + "KTT")
            S_.O32 = S_.rP; S_.tO32 = S_.trP
            S_.OSQ = S_.KT; S_.tOSQ = S_.tKT
            S_.BM = v3(S_.Qs, nt); S_.tBM = S_.tQs
            S_.tBM2 = T(nm + "BM2")
            S_.RSs = carve(16, F32); S_.tRSs = T(nm + "RSs")
            S_.D1 = carve(wd, F32); S_.tD1 = T(nm + "D1")
            S_.toks = [S_.tFa, S_.tFk, S_.tP, S_.trP, S_.tQs, S_.tKD, S_.tKT, S_.tQD, S_.tVT, S_.tSG, S_.tAM, S_.tKTT, S_.tRSs, S_.tD1]
            return S_

        HS = [mk_set(512, 4, "A"), mk_set(512, 4, "B")]
        SS = mk_set(128, 1, "S")
        for i_, S_ in enumerate(HS):
            S_.SBFC = v3(carve(9 * 128, BF16), 9); S_.tSBFC = T("SBFC%d" % i_)
            S_.kvb = (4 + 2 * i_, 5 + 2 * i_)
            S_.sidx = i_
            S_.toks.append(S_.tSBFC)
        S0 = v3(carve(16 * 128, F32), 16); tS0 = T("S0")
        S0B = v3(carve(16 * 128, BF16), 16); tS0B = T("S0B")
        VEXP = v3(carve(16 * 128, BF16), 16); tVEXP = T("VEXP")
        hg_toks = HS[0].toks + HS[1].toks + SS.toks + [tS0, tS0B, tVEXP]
        rc_end = apos[0]
        apos[0] = rc_off
        ZP = carve(528, F32); tZP = T("ZP")
        PA = carve(528, F32); tPA = T("PA")
        PB = carve(528, F32); tPB = T("PB")
        ZPS = v3(carve(16 * 24, F32), 16); tZPS = T("ZPS")
        PAS = v3(carve(16 * 24, F32), 16); tPAS = T("PAS")
        PBS = v3(carve(16 * 24, F32), 16); tPBS = T("PBS")
        MT = carve(TL, BF16); tMT = T("MT")
        T16 = carve(16, F32); tT16 = T("T16")
        pl_toks = [tZP, tPA, tPB, tZPS, tPAS, tPBS, tMT, tT16]
        rc_end = max(rc_end, apos[0])
        apos[0] = rc_off
        RB = v3(carve(8 * 512, BF16), 8); tRB = T("RB")
        SQ = v3(carve(8 * 512, BF16), 8); tSQ = T("SQ")
        ST = [carve(512, F32) for _ in range(4)]; tST = [T("ST%d" % i) for i in range(4)]
        TA = [carve(512, F32) for _ in range(4)]; tTA = [T("TA%d" % i) for i in range(4)]
        TM = [carve(512, F32) for _ in range(2)]; tTM = [T("TM%d" % i) for i in range(2)]
        tRBk = [T("RB%d" % i) for i in range(8)]; tSQk = [T("SQ%d" % i) for i in range(8)]
        ln_toks = tRBk + tSQk + tST + tTA + tTM
        rc_end = max(rc_end, apos[0])
        apos[0] = rc_end
        STG = [carve(2048, F32) for _ in range(2)]; tSTG = [T("STG%d" % i) for i in range(2)]
        WBF = [carve(2048, BF16) for _ in range(6)]; tWBF = [T("WBF%d" % i) for i in range(6)]
        CONS = carve(464, F32); tCONS = T("CONS")
        SMALL = carve(64, F32); tSMALL = T("SMALL")
        IDB = carve(128, BF16); ONESB = carve(128, BF16); tCB = T("CB")
        SEQMB = carve(16, BF16)
        NHALF = carve(8, F32); ZERO = carve(64, F32); DUMMY = carve(8, F32); tDUM = T("DUM")
        ONES32 = carve(128, F32)
        CTF = v3(carve(8 * 17, F32), 8); tCTF = T("CTF")
        SCT = v3(carve(8 * 17, BF16), 8); tSCT = T("SCT")
        BADAT = carve(48, F32); tBADAT = T("BADAT")
        MODT = v3(carve(48 * 17, F32), 48); tMODT = T("MODT")
        CO = carve(40, F32); tCO = T("CO")
        WPGS = carve(512, F32); tWPGS = T("WPGS")
        WPG = v3(carve(512, BF16), 4); tWPG = T("WPG")
        HIST = v3(carve(4 * 16, F32), 4); tHIST = T("HIST")
        S32 = v3(carve(8 * 128, F32), 8); tS32 = T("S32")
        S32b = v3(carve(8 * 128, F32), 8)
        S32x = [S32, S32b]
        tS32h = [[T("S32_%d_%d" % (b_, h_)) for h_ in range(8)] for b_ in range(2)]
        print("arena bytes used per partition:", apos[0])

        MASKP = CONS[:, 128:256]
        MASKS = CONS[:, 256:384]
        INVC = v3(CONS[:, 400:464], 4)
        PSC = SMALL[:, 0:4]
        LG1, LB1, LG2, LB2 = SMALL[:, 28:36], SMALL[:, 36:44], SMALL[:, 44:52], SMALL[:, 52:60]
        C0, C1, NC1, NW2 = CO[:, 0:8], CO[:, 8:16], CO[:, 16:24], CO[:, 24:32]

        PS = [es.enter_context(nc.psum_tensor("PS%d" % i, [128, 512], F32)) for i in range(8)]
        tPS = [T("PS%d" % i) for i in range(8)]
        prot = [0]

        pnb = [4]

        def pbank():
            i = prot[0] % pnb[0]
            prot[0] += 1
            return PS[i], tPS[i]

        def fence(toks):
            K.op("dve", lambda e: e.memset(DUMMY[:, 0:1], 0.0), (), list(toks) + [tDUM])

        def TS(out, in0, s1, s2, op0, op1=None, r=(), w=(), eng="dve"):
            if op1 is None:
                K.op(eng, lambda e: e.tensor_scalar(out=out, in0=in0, scalar1=s1, scalar2=None, op0=op0), r, w)
            else:
                K.op(eng, lambda e: e.tensor_scalar(out=out, in0=in0, scalar1=s1, scalar2=s2, op0=op0, op1=op1), r, w)

        def TT(out, in0, in1, op, r=(), w=(), eng="dve"):
            K.op(eng, lambda e: e.tensor_tensor(out=out, in0=in0, in1=in1, op=op), r, w)

        def STT(out, in0, sc, in1, op0, op1, r=(), w=()):
            K.op("dve", lambda e: e.scalar_tensor_tensor(out=out, in0=in0, scalar=sc, in1=in1, op0=op0, op1=op1), r, w)

        def ACT(out, in_, func, r=(), w=(), scale=1.0, bias=None):
            if bias is None:
                K.op("act", lambda e: e.activation(out=out, in_=in_, func=func, scale=scale), r, w)
            else:
                K.op("act", lambda e: e.activation(out=out, in_=in_, func=func, scale=scale, bias=bias), r, w)

        def MM(out, lhsT, rhs, start, stop, r=(), w=()):
            K.op("pe", lambda e: e.matmul(out, lhsT=lhsT, rhs=rhs, start=start, stop=stop), r, w)

        def TRP(out, in_, r=(), w=()):
            K.op("pe", lambda e: e.transpose(out=out, in_=in_, identity=IDB), r, w)

        def SCAN(out, d0, d1, r=(), w=()):
            K.op("dve", lambda e: e.tensor_tensor_scan(out=out, data0=d0, data1=d1, initial=1.0, op0=ALU.mult, op1=ALU.add), r, w)

        def CPY(out, in_, r=(), w=(), eng="dve"):
            K.op(eng, lambda e: e.tensor_copy(out=out, in_=in_), r, w)

        def MSET(out, val, w=(), eng="dve"):
            K.op(eng, lambda e: e.memset(out, val), (), w)

        I32 = mybir.dt.int32

        def RSQRT(y, x, t, hx, ty, tx, tt, thx, iters=3):
            TS(y.bitcast(I32), x.bitcast(I32), 1, None, ALU.arith_shift_right, r=[tx], w=[ty])
            TS(y.bitcast(I32), y.bitcast(I32), -1, 0x5f3759df, ALU.mult, ALU.add, r=[ty], w=[ty])
            TS(hx, x, -0.5, None, ALU.mult, r=[tx], w=[thx])
            for _ in range(iters):
                TT(t, y, y, ALU.mult, r=[ty], w=[tt])
                TT(t, t, hx, ALU.mult, r=[tt, thx], w=[tt])
                STT(y, t, 1.5, y, ALU.add, ALU.mult, r=[tt, ty], w=[ty])

        def RCP(out, in_, r=(), w=()):
            K.op("dve", lambda e: e.reciprocal(out=out, in_=in_), r, w)

        wq = []
        wloaded = []
        wctr = [0]

        sctr = [0]
        scr_tok = {}
        spill_q = []

        def flush_spill():
            _, sid_, nel_, b_, bt_ = spill_q.pop(0)
            K.dma(d_scr[sid_][:, :nel_], b_[:, :nel_], bt_, False, w=[scr_tok[sid_]])

        def _load(ap, nel, sid=None, from_scr=False):
            i = wctr[0]
            wctr[0] += 1
            b, bt = WBF[i % 6], tWBF[i % 6]
            if from_scr:
                while spill_q:
                    flush_spill()
                K.dma(b[:, :nel], d_scr[sid][:, :nel], bt, True, r=[scr_tok[sid]])
                return b, bt
            j = sctr[0]
            sctr[0] += 1
            s_, st = STG[j % 2], tSTG[j % 2]
            K.dma(s_[:, :nel], ap, st, True)
            ACT(b[:, :nel], s_[:, :nel], AF.Copy, r=[st], w=[bt])
            if sid is not None:
                scr_tok[sid] = T("scr%d" % sid)
                spill_q.append((i, sid, nel, b, bt))
            while spill_q and i - spill_q[0][0] >= 2:
                flush_spill()
            return b, bt

        def sched(lst):
            wq.extend(lst)

        def next_unit(lookahead=2):
            while len(wloaded) < 1 + lookahead and wq:
                wloaded.append(_load(*wq.pop(0)))
            return wloaded.pop(0)

        def proj(wv, wt, kc, c0, actv, at, col0, n):
            pb, pt = pbank()
            for k in range(kc):
                MM(pb[:, :n], wv[:, k, c0:c0 + 128], actv[:, k, col0:col0 + n], k == 0, k == kc - 1, r=[wt, at], w=[pt])
            return pb, pt

        K.dma(CONS, d_consts, tCONS, True)
        K.dma(SMALL, d_small, tSMALL, True)
        K.dma(CTF, d_cT, tCTF, True)
        K.dma(BADAT, d_badaT, tBADAT, True)
        K.dma(WPGS, d_wpg, tWPGS, True)
        CPY(IDB, CONS[:, 0:128], r=[tCONS], w=[tCB])
        MSET(ONESB, 1.0, w=[tCB])
        MSET(ONES32, 1.0, w=[tCB])
        CPY(SEQMB, CONS[:, 384:400], r=[tCONS], w=[tCB])
        MSET(NHALF, -0.5, w=[tCB])
        MSET(ZERO, 0.0, w=[tCB])
        MSET(S32, 0.0, w=tS32h[0])
        MSET(HIST, 0.0, w=[tHIST])
        ACT(WPG, v3(WPGS, 4), AF.Copy, r=[tWPGS], w=[tWPG])
        ACT(SCT, CTF, AF.Silu, r=[tCTF], w=[tSCT])
        TMP8 = CO[:, 32:40]
        TT(TMP8, SMALL[:, 4:12], SMALL[:, 12:20], ALU.subtract, r=[tSMALL], w=[tCO])
        ACT(TMP8, TMP8, AF.Tanh, r=[tCO], w=[tCO], scale=0.5)
        TS(C0, TMP8, 0.25, 0.75, ALU.mult, ALU.add, r=[tCO], w=[tCO])
        TS(C1, TMP8, -0.25, 0.25, ALU.mult, ALU.add, r=[tCO], w=[tCO])
        TS(NC1, TMP8, 0.25, -0.25, ALU.mult, ALU.add, r=[tCO], w=[tCO])
        TS(NW2, SMALL[:, 20:28], float(np.sqrt(128.0)), None, ALU.mult, r=[tSMALL], w=[tCO])

        def mod_units(u0, u1):
            for u in range(u0, u1):
                wb_, wt_ = next_unit()
                wv = v3(wb_, 8)
                for jj in range(2):
                    pk = 2 * u + jj
                    pb, pt = proj(wv, wt_, 8, jj * 128, SCT, tSCT, 0, 17)
                    TS(MODT[:, pk, :], pb[:, 0:17], BADAT[:, pk:pk + 1], None, ALU.add, r=[pt, tBADAT], w=[tMODT])

        sched([(d_wada[u], 2048, None, False) for u in range(8)])
        mod_units(0, 8)
        TS(MODT[:, 8:16, :], MODT[:, 8:16, :], 1.0, None, ALU.add, r=[tMODT], w=[tMODT])
        EPS2 = LN_EPS / (ALPHA * ALPHA)

        def mcol(pk):
            return MODT[:, pk, 0:1]

        def mrow(pk):
            return MODT[:, pk, 1:17].unsqueeze(2).to_broadcast([128, 16, 8])

        s3 = lambda a: a.rearrange("p (a b) -> p a b", a=16)

        def layer_norm_inplace(c0, n):
            xs = X1[:, :, c0:c0 + n]
            ntile = n // 128
            for k in range(8):
                ACT(RB[:, k, :n], X1[:, k, c0:c0 + n], AF.Copy, r=[tX1], w=[tRBk[k]])
                TT(SQ[:, k, :n], X1[:, k, c0:c0 + n], X1[:, k, c0:c0 + n], ALU.mult, r=[tX1], w=[tSQk[k]])
                MM(PS[6][:, :n], ONESB, RB[:, k, :n], k == 0, k == 7, r=[tCB, tRBk[k]], w=[tPS[6]])
            mean, msq, var, rstd = [a_[:, :n] for a_ in ST]
            TS(mean, PS[6][:, :n], 1.0 / 1024, None, ALU.mult, r=[tPS[6]], w=[tST[0]])
            TT(xs, xs, mean.unsqueeze(1).to_broadcast([128, 8, n]), ALU.subtract, r=[tX1, tST[0]], w=[tX1])
            for ti in range(ntile):
                for k in range(8):
                    MM(PS[7][:, ti:ti + 1], RB[:, k, ti * 128:(ti + 1) * 128], ONESB[:, 0:1], k == 0, k == 7, r=[tRBk[k], tCB], w=[tPS[7]])
            for ti in range(ntile):
                for k in range(8):
                    MM(PS[7][:, 4 + ti:5 + ti], SQ[:, k, ti * 128:(ti + 1) * 128], ONESB[:, 0:1], k == 0, k == 7, r=[tSQk[k], tCB], w=[tPS[7]])
            tl = ST[2]
            tm_, tv_, ty_, tt_, th_, tlo_ = [tl[:, i * 4:i * 4 + ntile] for i in range(6)]
            thi_ = tl[:, 24:26].bitcast(BF16)[:, 0:ntile]
            tk = tST[2]
            TS(tm_, PS[7][:, 0:ntile], 1.0 / 1024, None, ALU.mult, r=[tPS[7]], w=[tk])
            TT(tt_, tm_, tm_, ALU.mult, r=[tk], w=[tk])
            STT(tv_, PS[7][:, 4:4 + ntile], 1.0 / 1024, tt_, ALU.mult, ALU.subtract, r=[tPS[7], tk], w=[tk])
            TS(tv_, tv_, EPS2, None, ALU.add, r=[tk], w=[tk])
            RSQRT(ty_, tv_, tt_, th_, tk, tk, tk, tk)
            CPY(thi_, ty_, r=[tk], w=[tk])
            TT(tlo_, ty_, thi_, ALU.subtract, r=[tk], w=[tk])
            bmv = ST[1][:, :].bitcast(BF16)
            nn = ntile * 128
            bmh = bmv[:, 0:nn].rearrange("p (a b) -> p a b", a=ntile)
            bml = bmv[:, nn:2 * nn].rearrange("p (a b) -> p a b", a=ntile)
            IDN32_ = CONS[:, 0:128]
            TT(bmh, IDN32_.unsqueeze(1).to_broadcast([128, ntile, 128]), thi_.unsqueeze(2).to_broadcast([128, ntile, 128]), ALU.mult,
               r=[tCONS, tk], w=[tST[1]], eng="pool")
            TT(bml, IDN32_.unsqueeze(1).to_broadcast([128, ntile, 128]), tlo_.unsqueeze(2).to_broadcast([128, ntile, 128]), ALU.mult,
               r=[tCONS, tk], w=[tST[3]])
            MM(PS[6][:, :n], ONESB, bmv[:, 0:nn], True, False, r=[tCB, tST[1]], w=[tPS[6]])
            MM(PS[6][:, :n], ONESB, bmv[:, nn:2 * nn], False, True, r=[tCB, tST[3]], w=[tPS[6]])
            TT(xs, xs, PS[6][:, :n].unsqueeze(1).to_broadcast([128, 8, n]), ALU.mult, r=[tX1, tPS[6]], w=[tX1])

        mctr = [0]
        pending_out = []

        xctr = [0]

        def load_x_chunk(k, q):
            i = xctr[0] % 3
            xctr[0] += 1
            b0, pl = PASSES[q]
            K.dma(XS[i][:, 0:pl], d_xT[:, k, b0:b0 + pl], tXS[i], True)
            if q == NPASS - 1:
                K.dma(XS[i][:, pl:pl + 128], d_xT[:, k, 2048:2176], tXS[i], True)
            return XS[i], tXS[i]

        def phase1(q):
            fence([tUT] + tHT_all)
            for k in range(8):
                xs_k, txs = load_x_chunk(k, q)
                pl = PASSES[q][1]
                TS(UT[:, k, 0:pl], xs_k[:, 0:pl], mcol(8 + k), mcol(k), ALU.mult, ALU.add, r=[txs, tMODT], w=[tUT])
                if q == NPASS - 1:
                    TT(s3(TM[0][:, 0:128]), s3(xs_k[:, pl:pl + 128]), mrow(8 + k), ALU.mult, r=[txs, tMODT], w=[tTM[0]])
                    TT(s3(UT[:, k, pl:pl + 128]), s3(TM[0][:, 0:128]), mrow(k), ALU.add, r=[tTM[0], tMODT], w=[tUT])

        def sched_pass(q_):
            seq = []
            for h in range(8):
                seq += [(d_win[2 + 2 * h], 2048), (d_win[3 + 2 * h], 2048)]
            ada_at = len(seq)
            seq += [(d_win[0], 2048), (d_win[1], 2048)]
            for u in range(4):
                seq += [(d_wa[u], 1024), (d_win[18 + u], 2048), (d_wb[u], 2048), (d_win[22 + u], 2048)]
            seq += [(d_wo[u], 2048) for u in range(4)]
            for u in range(11):
                seq += [(d_wg[u], 2048), (d_wu[u], 2048)]
            seq += [(d_wd[u], 1408) for u in range(16)]
            assert len(seq) == NSCR
            seq = [(ap, nel, sid, USE_SCR and q_ > 0) for sid, (ap, nel) in enumerate(seq)]
            if not USE_SCR:
                seq = [(ap, nel, None, False) for (ap, nel, _, _) in seq]
            if q_ == 0:
                seq = seq[:ada_at] + [(d_wada[u], 2048, None, False) for u in range(8, 24)] + seq[ada_at:]
            sched(seq)

        for ps_ in range(NPASS):
            last = ps_ == NPASS - 1
            base, plen = PASSES[ps_]
            ptgs = [(c_, min(512, plen - c_), False) for c_ in range(0, plen, 512)]
            tgs = ptgs + ([(plen, 128, True)] if last else [])
            sched_pass(ps_)

            if ps_ == 0:
                phase1(0)
            fence(ln_toks + hg_toks + [tAT, tYHG, tMRG, tYPOOL])
            pnb[0] = 4

            for S_ in HS + [SS]:
                MSET(S_.D1, 0.0, w=[S_.tD1], eng="pool")
            IDN32 = CONS[:, 0:128]
            units = {}

            def head_stream(h, B, c0, n, is_s):
                qf, tqf, vo, tvo = units[h]
                ntile = n // 128
                L, nch = (8, 16) if is_s else (64, n // 64)
                pf, tpf = proj(qf, tqf, 8, 128, UT, tUT, c0, n)
                ACT(B.Fa[:, :n], pf[:, :n], AF.Tanh, r=[tpf], w=[B.tFa], scale=0.5)
                cs_ = lambda a_: a_.rearrange("p (a b) -> p a b", a=nch)[:, :, 0:1]
                D1b, tD1b = (SS.D1, SS.tD1) if is_s else (B.D1, B.tD1)
                ACT(B.rP[:, :n], B.Fa[:, :n], AF.Identity, r=[B.tFa, tCO], w=[B.trP], scale=C1[:, h:h + 1], bias=C0[:, h:h + 1])
                ACT(cs_(D1b[:, :n]), cs_(B.rP[:, :n]), AF.Identity, r=[B.trP], w=[tD1b])
                ACT(cs_(B.rP[:, :n]), cs_(B.rP[:, :n]), AF.Identity, r=[B.trP], w=[B.trP], scale=0.0)
                yield
                SCAN(B.P[:, :n], B.rP[:, :n], D1b[:, :n], r=[B.trP, tD1b], w=[B.tP])
                pq, tpq = proj(qf, tqf, 8, 0, UT, tUT, c0, n)
                ACT(B.Qs[:, :n], pq[:, :n], AF.Silu, r=[tpq], w=[B.tQs]); yield
                ACT(B.Fk[:, :n], B.Fa[:, :n], AF.Identity, r=[B.tFa, tCO], w=[B.tFk], scale=NC1[:, h:h + 1], bias=C1[:, h:h + 1])
                pog, tpog = proj(vo, tvo, 8, 128, UT, tUT, c0, n)
                ACT(B.SG[:, :n], pog[:, :n], AF.Silu, r=[tpog], w=[B.tSG]); yield
                pv, tpv = pbank()
                for ti in range(ntile):
                    for k in range(8):
                        MM(pv[:, ti * 128:(ti + 1) * 128], UT[:, k, c0 + ti * 128:c0 + (ti + 1) * 128], vo[:, k, 0:128], k == 0, k == 7, r=[tUT, tvo], w=[tpv])
                ACT(B.VT[:, :ntile, :], v3(pv[:, :n], ntile), AF.Copy, r=[tpv], w=[B.tVT])
                yield
                RCP(B.rP[:, :n], B.P[:, :n], r=[B.tP], w=[B.trP]); yield
                TT(B.KD[:, :n], B.Fk[:, :n], B.rP[:, :n], ALU.mult, r=[B.tFk, B.trP], w=[B.tKD])
                c3 = lambda a_: a_.rearrange("p (a b) -> p a b", a=nch)
                TT(c3(B.KT[:, :n]), c3(B.KD[:, :n]), c3(B.P[:, :n])[:, :, L - 1:L].to_broadcast([128, nch, L]), ALU.mult, r=[B.tKD, B.tP], w=[B.tKT], eng="pool")
                TT(B.QD[:, :n], B.Qs[:, :n], B.P[:, :n], ALU.mult, r=[B.tQs, B.tP], w=[B.tQD])
                yield
                pat, tpat = pbank() if KVMODE == "batch" else (PS[4], tPS[4])
                for ti in range(ntile):
                    MM(pat[:, ti * 128:(ti + 1) * 128], B.KD[:, ti * 128:(ti + 1) * 128], B.QD[:, ti * 128:(ti + 1) * 128], True, True, r=[B.tKD, B.tQD], w=[tpat])
                MK = MASKS if is_s else MASKP
                TT(B.AM[:, :ntile, :], v3(pat[:, :n], ntile), MK.unsqueeze(1).to_broadcast([128, ntile, 128]), ALU.mult, r=[tpat, tCONS], w=[B.tAM])
                pk_, tpk = pbank()
                pkb = pk_[:].bitcast(BF16).rearrange("p (a b) -> p a b", a=8)
                for ti in range(ntile):
                    TRP(pkb[:, ti, :], B.KT[:, ti * 128:(ti + 1) * 128], r=[B.tKT, tCB], w=[tpk])
                ACT(B.KTT[:, :ntile, :], pkb[:, :ntile, :], AF.Copy, r=[tpk], w=[B.tKTT])
                yield
                if not is_s:
                    ACT(B.SBFC[:, 0, :], S32x[0][:, h, :], AF.Copy, r=[tS32h[0][h]], w=[B.tSBFC])
                    if KVMODE == "batch":
                        for c in range(nch):
                            ti, hf = c // 2, c % 2
                            bi = B.kvb[c // 4]
                            sl = PS[bi][:, (c % 4) * 128:(c % 4 + 1) * 128]
                            MM(sl, B.KTT[hf * 64:(hf + 1) * 64, ti, :], B.VT[hf * 64:(hf + 1) * 64, ti, :], True, True, r=[B.tKTT, B.tVT], w=[tPS[bi]])
                        yield
                    def kv_mm(c):
                        ti, hf = c // 2, c % 2
                        bi = 6 + c % 2
                        sl = PS[bi][:, (2 * B.sidx + (c // 2) % 2) * 128:(2 * B.sidx + (c // 2) % 2 + 1) * 128]
                        MM(sl, B.KTT[hf * 64:(hf + 1) * 64, ti, :], B.VT[hf * 64:(hf + 1) * 64, ti, :], True, True, r=[B.tKTT, B.tVT], w=[tPS[bi]])

                    if KVMODE != "batch":
                        kv_mm(0)
                        kv_mm(1)
                    for c in range(nch):
                        if KVMODE == "batch":
                            bi = B.kvb[c // 4]
                            sl = PS[bi][:, (c % 4) * 128:(c % 4 + 1) * 128]
                        else:
                            bi = 6 + c % 2
                            sl = PS[bi][:, (2 * B.sidx + (c // 2) % 2) * 128:(2 * B.sidx + (c // 2) % 2 + 1) * 128]
                        si, so = c % 2, (c + 1) % 2
                        STT(S32x[so][:, h, :], S32x[si][:, h, :], B.P[:, c * 64 + 63:c * 64 + 64], sl, ALU.mult, ALU.add,
                            r=[tS32h[si][h], B.tP, tPS[bi]], w=[tS32h[so][h]])
                        ACT(B.SBFC[:, c + 1, :], S32x[so][:, h, :], AF.Copy, r=[tS32h[so][h]], w=[B.tSBFC])
                        if KVMODE != "batch" and c + 2 < nch:
                            kv_mm(c + 2)
                        yield
                    po, tpo = pbank() if KVMODE == "batch" else (PS[5], tPS[5])
                    for ti in range(ntile):
                        MM(po[:, ti * 128:(ti + 1) * 128], B.VT[:, ti, :], B.AM[:, ti, :], True, False, r=[B.tVT, B.tAM], w=[tpo])
                        for hf in range(2):
                            c = 2 * ti + hf
                            MM(po[:, c * 64:(c + 1) * 64], B.SBFC[:, c, :], B.QD[:, c * 64:(c + 1) * 64], False, hf == 1, r=[B.tSBFC, B.tQD], w=[tpo])
                    ACT(B.OSQ[:, :n], po[:, :n], AF.Square, r=[tpo], w=[B.tOSQ])
                    ACT(B.O32[:, :n], po[:, :n], AF.Identity, r=[tpo, tCO], w=[B.tO32], scale=NW2[:, h:h + 1])
                    yield
                else:
                    K.dma(S0, d_shg[:, h].rearrange("n k v -> k n v"), tS0, True)
                    ACT(S0B, S0, AF.Copy, r=[tS0], w=[tS0B]); yield
                    po, tpo = pbank() if KVMODE == "batch" else (PS[5], tPS[5])
                    MM(po[:, 0:128], B.VT[:, 0, :], B.AM[:, 0, :], True, False, r=[B.tVT, B.tAM], w=[tpo])
                    for sq_ in range(16):
                        MM(po[:, sq_ * 8:(sq_ + 1) * 8], S0B[:, sq_, :], B.QD[:, sq_ * 8:(sq_ + 1) * 8], False, sq_ == 15, r=[tS0B, B.tQD], w=[tpo])
                    ACT(B.O32[:, :n], po[:, :n], AF.Identity, r=[tpo, tCO], w=[B.tO32], scale=NW2[:, h:h + 1])
                    ACT(B.OSQ[:, :n], po[:, :n], AF.Square, r=[tpo], w=[B.tOSQ])
                    yield
                    TT(VEXP, B.VT[:, 0, :].unsqueeze(1).to_broadcast([128, 16, 128]), SEQMB.unsqueeze(2).to_broadcast([128, 16, 128]), ALU.mult, r=[B.tVT, tCB], w=[tVEXP])
                    Ps3 = B.P[:, :128].rearrange("p (a b) -> p a b", a=16)
                    TT(S0, S0, Ps3[:, :, 7:8].to_broadcast([128, 16, 128]), ALU.mult, r=[tS0, B.tP], w=[tS0], eng="pool")
                    yield
                    for qd_ in range(4):
                        pbk, tbk = pbank()
                        MM(pbk[:, :], B.KTT[:, 0, :], VEXP[:, qd_ * 4:(qd_ + 1) * 4, :], True, True, r=[B.tKTT, tVEXP], w=[tbk])
                        TT(S0[:, qd_ * 4:(qd_ + 1) * 4, :], S0[:, qd_ * 4:(qd_ + 1) * 4, :], v3(pbk[:, :], 4), ALU.add, r=[tS0, tbk], w=[tS0])
                        yield
                    K.dma(o_nhs[:, h].rearrange("n k v -> k n v"), S0, tS0, False)
                pss, tpss = pbank()
                for ti in range(ntile):
                    MM(pss[:, ti:ti + 1], B.OSQ[:, ti * 128:(ti + 1) * 128], ONESB[:, 0:1], True, True, r=[B.tOSQ, tCB], w=[tpss])
                xs_, ys_, ts_, hs_ = [B.RSs[:, i * 4:i * 4 + ntile] for i in range(4)]
                TS(xs_, pss[:, 0:ntile], 128.0 * RMS_EPS, None, ALU.add, r=[tpss], w=[B.tRSs]); yield
                TS(ys_.bitcast(I32), xs_.bitcast(I32), 1, None, ALU.arith_shift_right, r=[B.tRSs], w=[B.tRSs])
                TS(ys_.bitcast(I32), ys_.bitcast(I32), -1, 0x5f3759df, ALU.mult, ALU.add, r=[B.tRSs], w=[B.tRSs])
                TS(hs_, xs_, -0.5, None, ALU.mult, r=[B.tRSs], w=[B.tRSs]); yield
                for _ in range(3):
                    TT(ts_, ys_, ys_, ALU.mult, r=[B.tRSs], w=[B.tRSs])
                    TT(ts_, ts_, hs_, ALU.mult, r=[B.tRSs], w=[B.tRSs])
                    STT(ys_, ts_, 1.5, ys_, ALU.add, ALU.mult, r=[B.tRSs], w=[B.tRSs]); yield
                bmv = B.BM[:, :, :].rearrange("p a b -> p (a b)").bitcast(BF16)
                nn = ntile * 128
                bmh = bmv[:, 0:nn].rearrange("p (a b) -> p a b", a=ntile)
                bml = bmv[:, nn:2 * nn].rearrange("p (a b) -> p a b", a=ntile)
                hi_b = B.RSs[:, 8:8 + 2].bitcast(BF16)[:, 0:ntile]
                CPY(hi_b, ys_, r=[B.tRSs], w=[B.tRSs])
                TT(hs_, ys_, hi_b, ALU.subtract, r=[B.tRSs], w=[B.tRSs])
                TT(bmh, IDN32.unsqueeze(1).to_broadcast([128, ntile, 128]), hi_b.unsqueeze(2).to_broadcast([128, ntile, 128]), ALU.mult,
                   r=[tCONS, B.tRSs], w=[B.tBM], eng="pool")
                TT(bml, IDN32.unsqueeze(1).to_broadcast([128, ntile, 128]), hs_.unsqueeze(2).to_broadcast([128, ntile, 128]), ALU.mult,
                   r=[tCONS, B.tRSs], w=[B.tBM2])
                yield
                pbc, tpbc = pbank()
                MM(pbc[:, :n], ONESB, bmv[:, 0:nn], True, False, r=[tCB, B.tBM], w=[tpbc])
                MM(pbc[:, :n], ONESB, bmv[:, nn:2 * nn], False, True, r=[tCB, B.tBM2], w=[tpbc])
                TT(B.O32[:, :n], B.O32[:, :n], pbc[:, :n], ALU.mult, r=[B.tO32, tpbc], w=[B.tO32]); yield
                TT(YHG[:, h, c0:c0 + n], B.O32[:, :n], B.SG[:, :n], ALU.mult, r=[B.tO32, B.tSG], w=[tYHG])
                yield

            def get_units(h):
                qf_, tqf = next_unit()
                vo_, tvo = next_unit()
                units[h] = (v3(qf_, 8), tqf, v3(vo_, 8), tvo)

            def chain(*gs):
                for g_ in gs:
                    yield from g_

            def zip_run(gens):
                gens = list(gens)
                if not ZIP:
                    for g_ in gens:
                        for _ in g_:
                            pass
                    return
                while gens:
                    for g_ in list(gens):
                        try:
                            next(g_)
                        except StopIteration:
                            gens.remove(g_)

            sbusy = [False]

            def stream_chain(heads, B):
                for h_ in heads:
                    get_units(h_)
                    for (c0_, n_, _) in ptgs:
                        yield from head_stream(h_, B, c0_, n_, False)
                    if last:
                        while sbusy[0]:
                            yield
                        sbusy[0] = True
                        yield from head_stream(h_, B, plen, 128, True)
                        sbusy[0] = False

            gA = stream_chain([0, 2, 4, 6], HS[0])
            gB = stream_chain([1, 3, 5, 7], HS[1])
            for _ in range(STAGGER):
                next(gA)
            while pending_out:
                dst_, src_ = pending_out.pop(0)
                K.dma(dst_, src_, tX1, False)
            zip_run([gA, gB])
            if last:
                for h in range(8):
                    K.dma(o_nhp[h], S32[:, h, :], tS32h[0][h], False)
            fence(hg_toks + pl_toks)
            pnb[0] = 6
            if ps_ == 0:
                mod_units(8, 24)
                TS(MODT[:, 32:40, :], MODT[:, 32:40, :], 1.0, None, ALU.add, r=[tMODT], w=[tMODT])
                TS(MODT[:, 16:24, :], MODT[:, 16:24, :], 0.5 / ALPHA, None, ALU.mult, r=[tMODT], w=[tMODT])
                TS(MODT[:, 40:48, :], MODT[:, 40:48, :], 1.0 / ALPHA, None, ALU.mult, r=[tMODT], w=[tMODT])

            for g in range(4):
                if g % 2 == 0:
                    pw_, tpw = next_unit(); pw = v3(pw_, 8)
                w = 2 << g
                gc = (g % 2) * 128
                for (c0_, n_, _) in ptgs:
                    pz, tpz = proj(pw, tpw, 8, gc, UT, tUT, c0_, n_)
                    CPY(ZP[:, 0:16], HIST[:, g, :], r=[tHIST], w=[tZP])
                    ACT(ZP[:, 16:16 + n_], pz[:, :n_], AF.Copy, r=[tpz], w=[tZP])
                    CPY(HIST[:, g, :], ZP[:, n_:n_ + 16], r=[tZP], w=[tHIST])
                    cur, tcur = ZP, tZP
                    for j in range(g + 1):
                        sh = 1 << j
                        nx, tnx = (PA, tPA) if j % 2 == 0 else (PB, tPB)
                        TT(nx[:, sh:16 + n_], cur[:, sh:16 + n_], cur[:, 0:16 + n_ - sh], ALU.add, r=[tcur], w=[tnx])
                        cur, tcur = nx, tnx
                    STT(MT[:, c0_:c0_ + n_], cur[:, 16:16 + n_], 1.0 / w, ZP[:, 16:16 + n_], ALU.mult, ALU.subtract, r=[tcur, tZP], w=[tMT])
                    if ps_ == 0 and c0_ == 0:
                        TT(T16, cur[:, 16:32], INVC[:, g, :], ALU.mult, r=[tcur, tCONS], w=[tT16])
                        TT(MT[:, 0:16], T16, ZP[:, 16:32], ALU.subtract, r=[tT16, tZP, tMT], w=[tMT])
                    if last and c0_ + n_ == plen:
                        K.dma(o_zpp[:, g, :], ZP[:, n_ + 1:n_ + 16], tZP, False)
                if last:
                    MSET(ZPS, 0.0, w=[tZPS])
                    K.dma(ZPS[:, :, 1:16], d_spT[:, g], tZPS, True)
                    pzs, tpzs = proj(pw, tpw, 8, gc, UT, tUT, plen, 128)
                    ACT(ZPS[:, :, 16:24], s3(pzs[:, 0:128]), AF.Copy, r=[tpzs], w=[tZPS])
                    cur, tcur = ZPS, tZPS
                    for j in range(g + 1):
                        sh = 1 << j
                        nx, tnx = (PAS, tPAS) if j % 2 == 0 else (PBS, tPBS)
                        TT(nx[:, :, sh:24], cur[:, :, sh:24], cur[:, :, 0:24 - sh], ALU.add, r=[tcur], w=[tnx])
                        cur, tcur = nx, tnx
                    STT(s3(MT[:, plen:plen + 128]), cur[:, :, 16:24], 1.0 / w, ZPS[:, :, 16:24], ALU.mult, ALU.subtract, r=[tcur, tZPS, tMT], w=[tMT])
                    K.dma(o_zps[:, g], ZPS[:, :, 9:24], tZPS, False)
                for (c0, n, is_s) in tgs:
                    pb, pt = pbank()
                    MM(pb[:, :n], WPG[:, g, :], MT[:, c0:c0 + n], True, True, r=[tWPG, tMT], w=[pt])
                    ACT(YPOOL[:, g, c0:c0 + n], pb[:, :n], AF.Identity, r=[pt, tSMALL], w=[tYPOOL], scale=PSC[:, g:g + 1])
            fence(pl_toks + ln_toks)

            for u in range(4):
                wa_, twa = next_unit(); wa = v3(wa_[:, :1024], 4)
                ga_, tga = next_unit(); ga = v3(ga_, 8)
                wb_, twb = next_unit(); wbv = v3(wb_, 8)
                gb_, tgb = next_unit(); gb = v3(gb_, 8)
                for jj in range(2):
                    j = 2 * u + jj
                    for (c0, n, is_s) in tgs:
                        pa, tpa = proj(wa, twa, 4, jj * 128, YPOOL, tYPOOL, c0, n)
                        pg, tpg = proj(ga, tga, 8, jj * 128, UT, tUT, c0, n)
                        mctr[0] += 1
                        ia, ib = (mctr[0] % 2) * 2, (mctr[0] % 2) * 2 + 1
                        ACT(TA[ia][:, :n], pg[:, :n], AF.Tanh, r=[tpg], w=[tTA[ia]], scale=0.5)
                        STT(TM[0][:, :n], TA[ia][:, :n], 1.0, pa[:, :n], ALU.add, ALU.mult, r=[tTA[ia], tpa], w=[tTM[0]])
                        pb2, tpb2 = proj(wbv, twb, 8, jj * 128, YHG, tYHG, c0, n)
                        pg2, tpg2 = proj(gb, tgb, 8, jj * 128, UT, tUT, c0, n)
                        ACT(TA[ib][:, :n], pg2[:, :n], AF.Tanh, r=[tpg2], w=[tTA[ib]], scale=0.5)
                        STT(TM[1][:, :n], TA[ib][:, :n], 1.0, pb2[:, :n], ALU.add, ALU.mult, r=[tTA[ib], tpb2], w=[tTM[1]])
                        TT(MRG[:, j, c0:c0 + n], TM[0][:, :n], TM[1][:, :n], ALU.add, r=[tTM[0], tTM[1]], w=[tMRG], eng="pool")

            for u in range(4):
                wo_, two = next_unit(); wo = v3(wo_, 8)
                for jj in range(2):
                    j = 2 * u + jj
                    xs_j, txs_j = load_x_chunk(j, ps_)
                    for (c0, n, is_s) in tgs:
                        pm, tpm = proj(wo, two, 8, jj * 128, MRG, tMRG, c0, n)
                        if not is_s:
                            STT(X1[:, j, c0:c0 + n], pm[:, :n], mcol(16 + j), xs_j[:, c0:c0 + n], ALU.mult, ALU.add, r=[tpm, tMODT, txs_j], w=[tX1])
                        else:
                            TT(s3(TM[0][:, 0:128]), s3(pm[:, 0:128]), mrow(16 + j), ALU.mult, r=[tpm, tMODT], w=[tTM[0]])
                            TT(X1[:, j, c0:c0 + 128], TM[0][:, 0:128], xs_j[:, c0:c0 + 128], ALU.add, r=[tTM[0], txs_j], w=[tX1])
            fence([tUT] + tHT_all)
            for (c0, n, is_s) in tgs:
                layer_norm_inplace(c0, n)
                for k in range(8):
                    TS(X1[:, k, c0:c0 + n], X1[:, k, c0:c0 + n], LG1[:, k:k + 1], LB1[:, k:k + 1], ALU.mult, ALU.add, r=[tX1, tSMALL], w=[tX1])
                    if not is_s:
                        ACT(HT[:, k, c0:c0 + n], X1[:, k, c0:c0 + n], AF.Identity, r=[tX1, tMODT], w=[tHTg[c0]], scale=mcol(32 + k), bias=mcol(24 + k))
                    else:
                        TT(s3(TM[0][:, 0:128]), s3(X1[:, k, c0:c0 + 128]), mrow(32 + k), ALU.mult, r=[tX1, tMODT], w=[tTM[0]])
                        TT(s3(HT[:, k, c0:c0 + 128]), s3(TM[0][:, 0:128]), mrow(24 + k), ALU.add, r=[tTM[0], tMODT], w=[tHTg[c0]])
            fence([tYHG, tMRG, tYPOOL, tAT])

            for u in range(11):
                wg_, twg = next_unit(); wg = v3(wg_, 8)
                wu_, twu = next_unit(); wu = v3(wu_, 8)
                for jj in range(2):
                    fc = 2 * u + jj
                    for (c0, n, is_s) in tgs:
                        pg, tpg = proj(wg, twg, 8, jj * 128, HT, tHTg[c0], c0, n)
                        pu, tpu = proj(wu, twu, 8, jj * 128, HT, tHTg[c0], c0, n)
                        mctr[0] += 1
                        i2 = mctr[0] % 4
                        ACT(TA[i2][:, :n], pg[:, :n], AF.Silu, r=[tpg], w=[tTA[i2]])
                        TT(AT[:, fc, c0:c0 + n], TA[i2][:, :n], pu[:, :n], ALU.mult, r=[tTA[i2], tpu], w=[tAT])
            if not last:
                phase1(ps_ + 1)
            for j in range(8):
                wda_, twda = next_unit(); wda = v3(wda_[:, :1408], 11)
                wdb_, twdb = next_unit(); wdb = v3(wdb_[:, :1408], 11)
                for (c0, n, is_s) in tgs:
                    pd, tpd = pbank()
                    for kc in range(22):
                        wv, wt = (wda, twda) if kc < 11 else (wdb, twdb)
                        MM(pd[:, :n], wv[:, kc % 11, :], AT[:, kc, c0:c0 + n], kc == 0, kc == 21, r=[wt, tAT], w=[tpd])
                    if not is_s:
                        STT(X1[:, j, c0:c0 + n], pd[:, :n], mcol(40 + j), X1[:, j, c0:c0 + n], ALU.mult, ALU.add, r=[tpd, tMODT, tX1], w=[tX1])
                    else:
                        TT(s3(TM[0][:, 0:128]), s3(pd[:, 0:128]), mrow(40 + j), ALU.mult, r=[tpd, tMODT], w=[tTM[0]])
                        TT(X1[:, j, c0:c0 + 128], TM[0][:, 0:128], X1[:, j, c0:c0 + 128], ALU.add, r=[tTM[0], tX1], w=[tX1])
            for (c0, n, is_s) in tgs:
                layer_norm_inplace(c0, n)
                for k in range(8):
                    TS(X1[:, k, c0:c0 + n], X1[:, k, c0:c0 + n], LG2[:, k:k + 1], LB2[:, k:k + 1], ALU.mult, ALU.add, r=[tX1, tSMALL], w=[tX1])
                dcol = 2048 if is_s else base + c0
                if last:
                    K.dma(o_yT[:, :, dcol:dcol + n], X1[:, :, c0:c0 + n], tX1, False)
                else:
                    pending_out.append((o_yT[:, :, dcol:dcol + n], X1[:, :, c0:c0 + n]))

        final = [tX1, tZP, tZPS, tS0] + tS32h[0]
        K.emit(es, final_toks=[t for t in final if t.dsem is not None])
    return nc


def _units(w, kc, ncol):
    Kd, N = w.shape
    assert Kd == kc * 128
    nu = N // ncol
    return np.ascontiguousarray(w.reshape(kc, 128, nu, ncol).transpose(2, 1, 0, 3).reshape(nu, 128, kc * ncol))


_NC_CACHE = {}


def prep_inputs(x_prompt, x_sample, state_pool, state_hgrn, c_prompt, c_sample,
                w_ada, b_ada, w_in, w_pool_grp, pool_scale, lb_logits, hgrn_norm_w,
                w_a, w_b, w_out, ln1_g, ln1_b, w_gate, w_up, w_down, ln2_g, ln2_b, cores=None):
    f = lambda a: np.asarray(a, dtype=np.float32)
    x_prompt, x_sample, state_pool, state_hgrn = f(x_prompt), f(x_sample), f(state_pool), f(state_hgrn)
    c_prompt, c_sample = f(c_prompt), f(c_sample)
    w_ada, b_ada, w_in = f(w_ada)[0], f(b_ada)[0], f(w_in)[0]
    w_pool_grp, pool_scale, lb_logits, hgrn_norm_w = f(w_pool_grp)[0], f(pool_scale)[0], f(lb_logits), f(hgrn_norm_w)[0]
    w_a, w_b, w_out = f(w_a)[0], f(w_b)[0], f(w_out)[0]
    w_gate, w_up, w_down = f(w_gate)[0], f(w_up)[0], f(w_down)[0]
    ln1_g, ln1_b, ln2_g, ln2_b = f(ln1_g)[0], f(ln1_b)[0], f(ln2_g)[0], f(ln2_b)[0]

    colT = lambda v, nk: np.ascontiguousarray(v.reshape(nk, 128).T)
    wada_u = _units(w_ada, 8, 256)
    Pq, Pf, Pi, Pog, Pga, Pgb = 512, 1536, 2560, 3584, 4608, 5632
    cols = list(range(0, 512))
    for h in range(8):
        cols += list(range(Pq + h * 128, Pq + (h + 1) * 128)) + list(range(Pf + h * 128, Pf + (h + 1) * 128))
        cols += list(range(Pi + h * 128, Pi + (h + 1) * 128)) + list(range(Pog + h * 128, Pog + (h + 1) * 128))
    cols += list(range(Pga, Pga + 1024)) + list(range(Pgb, Pgb + 1024))
    win_u = _units(w_in[:, np.array(cols)], 8, 256)
    wpg = np.ascontiguousarray(w_pool_grp.transpose(1, 0, 2).reshape(128, 512))
    wa_u = _units(w_a, 4, 256)
    wb_u = _units(w_b, 8, 256)
    wo_u = _units(w_out, 8, 256)
    wg_u = _units(w_gate, 8, 256)
    wu_u = _units(w_up, 8, 256)
    wd4 = w_down.reshape(2, 11, 128, 8, 128)
    wd_u = np.ascontiguousarray(wd4.transpose(3, 0, 2, 1, 4).reshape(16, 128, 1408))
    small = np.zeros((128, 64), np.float32)
    small[:, 0:4] = colT(pool_scale, 4)
    small[:, 4:12] = colT(lb_logits[0], 8)
    small[:, 12:20] = colT(lb_logits[1], 8)
    small[:, 20:28] = colT(hgrn_norm_w, 8)
    small[:, 28:36] = colT(ln1_g, 8)
    small[:, 36:44] = colT(ln1_b, 8)
    small[:, 44:52] = colT(ln2_g, 8)
    small[:, 52:60] = colT(ln2_b, 8)
    badaT = colT(b_ada, 48)
    consts = np.zeros((128, 464), np.float32)
    consts[:, 0:128] = np.eye(128, dtype=np.float32)
    s_i = np.arange(128)[:, None]
    t_i = np.arange(128)[None, :]
    consts[:, 128:256] = ((s_i // 64 == t_i // 64) & (s_i <= t_i)).astype(np.float32)
    consts[:, 256:384] = ((s_i // 8 == t_i // 8) & (s_i <= t_i)).astype(np.float32)
    consts[:, 384:400] = (s_i // 8 == np.arange(16)[None, :]).astype(np.float32)
    for g in range(4):
        wdw = 2 << g
        consts[:, 400 + g * 16:416 + g * 16] = (1.0 / np.minimum(np.arange(16) + 1, wdw)).astype(np.float32)[None, :]

    in_maps = []
    for c in (range(NCORE) if cores is None else cores):
        xs = x_sample[c * 16:(c + 1) * 16].reshape(128, 1024)
        xall = np.concatenate([x_prompt[c], xs], axis=0)
        xT = np.ascontiguousarray(xall.T.reshape(8, 128, 2176).transpose(1, 0, 2))
        call = np.concatenate([c_prompt[c:c + 1], c_sample[c * 16:(c + 1) * 16]], axis=0)
        cT = np.ascontiguousarray(call.T.reshape(8, 128, 17).transpose(1, 0, 2))
        sp = state_pool[0, c * 16:(c + 1) * 16]
        spT = np.ascontiguousarray(sp.reshape(16, 15, 4, 128).transpose(3, 2, 0, 1))
        shg = np.ascontiguousarray(state_hgrn[0, c * 16:(c + 1) * 16])
        in_maps.append({"xT": xT, "cT": cT, "badaT": badaT, "wada": wada_u, "win": win_u, "wpg": wpg, "wa": wa_u, "wb": wb_u,
                        "wo": wo_u, "wg": wg_u, "wu": wu_u, "wd": wd_u, "small": small, "consts": consts, "spT": spT, "shg": shg})
    return in_maps


def assemble_core(r):
    yT = np.asarray(r["yT"], dtype=np.float32)
    yall = yT.transpose(2, 1, 0).reshape(2176, 1024)
    y_p = yall[:2048]
    y_s = yall[2048:].reshape(16, 8, 1024)
    npp = np.asarray(r["zpp"], dtype=np.float32).transpose(2, 1, 0).reshape(15, 512)
    nps = np.asarray(r["zps"], dtype=np.float32).transpose(2, 3, 1, 0).reshape(16, 15, 512)
    return y_p, y_s, npp, np.asarray(r["nhp"], dtype=np.float32), nps, np.asarray(r["nhs"], dtype=np.float32)


def kernel(**inputs):
    in_maps = prep_inputs(**inputs)
    if "nc" not in _NC_CACHE:
        _NC_CACHE["nc"] = build_nc()
    res = run_bass_kernel_spmd(_NC_CACHE["nc"], in_maps, core_ids=list(range(NCORE)))
    R = res.results
    y_p = np.zeros((8, 2048, 1024), np.float32)
    y_s = np.zeros((128, 8, 1024), np.float32)
    npp = np.zeros((1, 8, 15, 512), np.float32)
    nhp = np.zeros((1, 8, 8, 128, 128), np.float32)
    nps = np.zeros((1, 128, 15, 512), np.float32)
    nhs = np.zeros((1, 128, 8, 128, 128), np.float32)
    for c in range(NCORE):
        a, b, d, e, g, h = assemble_core(R[c])
        y_p[c] = a
        y_s[c * 16:(c + 1) * 16] = b
        npp[0, c] = d
        nhp[0, c] = e
        nps[0, c * 16:(c + 1) * 16] = g
        nhs[0, c * 16:(c + 1) * 16] = h
    return (y_p, y_s, npp, nhp, nps, nhs)
```
